# Optimizing a Trainium2 kernel written in Bass

```python
import jax, jax.numpy as jnp
from jax import lax
import numpy as np

D_MODEL = 1024
BATCH = 16
SEQ = 2048
DEPTH = 1
DEC_BATCH = 16
DEC_SEQ = 16
PAST_LEN = 1024

CHUNK = 64
EPS = 1e-6
A_CHUNK = 128
A_GROUPS = 8
A_WIDTH = D_MODEL
A_GDIM = A_WIDTH // A_GROUPS
B_HEADS = 4
B_DK = 256
B_DV = 512
B_QK = B_HEADS * B_DK
B_V = B_HEADS * B_DV
ROPE_BASE = 10000.0
SPLITS = (A_WIDTH, 2 * A_WIDTH, 2 * A_WIDTH + B_QK, 2 * A_WIDTH + 2 * B_QK,
          2 * A_WIDTH + 2 * B_QK + B_V, 2 * A_WIDTH + 2 * B_QK + 2 * B_V)
IN_COLS = 2 * A_WIDTH + 2 * B_QK + 2 * B_V + 2 * D_MODEL
P_HEADS = 8
P_NKEYS = 128
P_EXPERTS = P_NKEYS * P_NKEYS
P_DKEY = 256
P_HALF = P_DKEY // 2
P_TOPK = 16
P_BLOCK = 256

kernel_name = "hybrid_sgu_retention_peer_stream_step"


def rmsnorm(x, g):
    xf = x.astype(jnp.float32)
    y = xf * lax.rsqrt(jnp.mean(xf * xf, axis=-1, keepdims=True) + EPS)
    return (y * g.astype(jnp.float32)).astype(x.dtype)


def _rms(x):
    xf = x.astype(jnp.float32)
    return (xf * lax.rsqrt(jnp.mean(xf * xf, axis=-1, keepdims=True) + EPS)).astype(x.dtype)


def _log_gamma():
    return jnp.log(1.0 - 2.0 ** (-5.0 - jnp.arange(B_HEADS, dtype=jnp.float32)))


def _rotary(x, pos):
    half = B_DK // 2
    inv = 1.0 / (ROPE_BASE ** jnp.linspace(0.0, 1.0, half, dtype=jnp.float32))
    ang = pos.astype(jnp.float32)[:, None] * inv[None, :]
    cos, sin = jnp.cos(ang), jnp.sin(ang)
    x1 = x[..., :half].astype(jnp.float32)
    x2 = x[..., half:].astype(jnp.float32)
    return jnp.concatenate([x1 * cos - x2 * sin, x1 * sin + x2 * cos], axis=-1).astype(x.dtype)


def _retention_block(q, k, v, s, log_g):
    L = q.shape[2]
    dt = q.dtype
    idx = jnp.arange(L, dtype=jnp.float32)
    diff = idx[:, None] - idx[None, :]
    lg = log_g[:, None]
    decay = jnp.where(diff[None] >= 0.0, jnp.exp(lg[:, :, None] * jnp.maximum(diff, 0.0)[None]), 0.0).astype(dt)
    scores = jnp.einsum('bhnd,bhmd->bhnm', q, k) * decay
    out = jnp.einsum('bhnm,bhmv->bhnv', scores, v)
    q_dec = jnp.exp(lg * (idx + 1.0)).astype(dt)
    out = out + jnp.einsum('bhnd,bhdv->bhnv', q * q_dec[None, :, :, None], s)
    k_dec = jnp.exp(lg * (L - 1.0 - idx)).astype(dt)
    s_new = jnp.exp(log_g * L).astype(dt)[None, :, None, None] * s + jnp.einsum(
        'bhmd,bhmv->bhdv', k * k_dec[None, :, :, None], v)
    return out, s_new


def _mixer(h, pos0, s0, log_g, w_in, w_s, b_s, g_sgu, w_proj_a, w_proj_b, b_gate, w_out):
    B, L, _ = h.shape
    z = h @ w_in
    u_a, v_a, q, k, v, g, gate = jnp.split(z, SPLITS, axis=-1)
    u_a = jax.nn.gelu(u_a)
    v_a = rmsnorm(jax.nn.gelu(v_a), g_sgu)
    ac = min(L, A_CHUNK)
    nc = L // ac
    vr = v_a.reshape(B, nc, ac, A_GROUPS, A_GDIM)
    ws = w_s[:, :ac, :ac] * jnp.tril(jnp.ones((ac, ac), h.dtype))[None]
    mixed = jnp.einsum('gnm,bcmgd->bcngd', ws, vr) + b_s[:, :ac].T[None, None, :, :, None]
    y_a = u_a * mixed.reshape(B, L, A_WIDTH)
    pos = pos0 + jnp.arange(L)
    qh = _rotary(q.reshape(B, L, B_HEADS, B_DK).transpose(0, 2, 1, 3), pos)
    kh = _rotary(k.reshape(B, L, B_HEADS, B_DK).transpose(0, 2, 1, 3), pos) * (B_DK ** -0.5)
    vh = v.reshape(B, L, B_HEADS, B_DV).transpose(0, 2, 1, 3)
    c = min(L, CHUNK)
    ncb = L // c
    to_blocks = lambda t: t.reshape(B, B_HEADS, ncb, c, t.shape[-1]).transpose(2, 0, 1, 3, 4)

    def step(s, blk):
        qc, kc, vc = blk
        o, s = _retention_block(qc, kc, vc, s, log_g)
        return s, o

    s_new, o = lax.scan(step, s0.astype(h.dtype), (to_blocks(qh), to_blocks(kh), to_blocks(vh)))
    o = o.transpose(1, 2, 0, 3, 4).reshape(B, B_HEADS, L, B_DV)
    o = _rms(o).transpose(0, 2, 1, 3).reshape(B, L, B_V)
    y_b = jax.nn.silu(g) * o
    gates = jax.nn.sigmoid(gate + b_gate)
    g_a, g_b = jnp.split(gates, 2, axis=-1)
    m = g_a * (y_a @ w_proj_a) + g_b * (y_b @ w_proj_b)
    return m @ w_out, s_new, v_a


def _peer_block(t, w_query, k1, k2, eu, ev):
    T = t.shape[0]
    q = (t @ w_query).reshape(T, P_HEADS, P_DKEY)
    s1 = jnp.einsum('thd,nd->thn', q[..., :P_HALF], k1).astype(jnp.float32)
    s2 = jnp.einsum('thd,nd->thn', q[..., P_HALF:], k2).astype(jnp.float32)
    v1, i1 = lax.top_k(s1, P_TOPK)
    v2, i2 = lax.top_k(s2, P_TOPK)
    cand = (v1[..., :, None] + v2[..., None, :]).reshape(T, P_HEADS, P_TOPK * P_TOPK)
    vals, flat = lax.top_k(cand, P_TOPK)
    e1 = jnp.take_along_axis(i1, flat // P_TOPK, axis=-1)
    e2 = jnp.take_along_axis(i2, flat % P_TOPK, axis=-1)
    experts = e1 * P_NKEYS + e2
    w = jax.nn.softmax(vals, axis=-1).astype(t.dtype)
    a = jax.nn.gelu(jnp.einsum('td,thkd->thk', t, eu[experts]))
    return jnp.einsum('thk,thkd->td', w * a, ev[experts])


def _peer(h, w_query, k1, k2, eu, ev):
    B, L, D = h.shape
    n = B * L
    blk = min(P_BLOCK, n)
    pad = (-n) % blk
    t = jnp.pad(h.reshape(n, D), ((0, pad), (0, 0))).reshape(-1, blk, D)
    out = lax.map(lambda tb: _peer_block(tb, w_query, k1, k2, eu, ev), t)
    return out.reshape(-1, D)[:n].reshape(B, L, D)


def _layer(x, pos0, s0, log_g, w_in, w_s, b_s, g_sgu, w_proj_a, w_proj_b, b_gate, w_out,
           g_mix, g_ffn, w_query, k1, k2, eu, ev):
    m, s_new, v_a = _mixer(rmsnorm(x, g_mix), pos0, s0, log_g, w_in, w_s, b_s, g_sgu,
                           w_proj_a, w_proj_b, b_gate, w_out)
    x = x + m
    x = x + _peer(rmsnorm(x, g_ffn), w_query, k1, k2, eu, ev)
    return x, s_new, v_a


def setup_inputs(seed: int = 0) -> dict:
    key = jax.random.key(seed)
    ks = jax.random.split(key, 21)
    f = jnp.float32
    nrm = lambda k, shape, scale: jax.random.normal(k, shape, f) * scale
    return {
        "x_prompt": nrm(ks[0], (BATCH, SEQ, D_MODEL), 1.0),
        "x_sample": nrm(ks[1], (DEC_BATCH, DEC_SEQ, D_MODEL), 1.0),
        "state_ret": nrm(ks[2], (DEPTH, DEC_BATCH, B_HEADS, B_DK, B_DV), 0.1),
        "w_in": nrm(ks[3], (DEPTH, D_MODEL, IN_COLS), D_MODEL ** -0.5),
        "w_s": nrm(ks[4], (DEPTH, A_GROUPS, A_CHUNK, A_CHUNK), A_CHUNK ** -0.5),
        "b_s": 1.0 + nrm(ks[5], (DEPTH, A_GROUPS, A_CHUNK), 0.02),
        "g_sgu": 1.0 + nrm(ks[6], (DEPTH, A_WIDTH), 0.02),
        "w_proj_a": nrm(ks[7], (DEPTH, A_WIDTH, D_MODEL), A_WIDTH ** -0.5),
        "w_proj_b": nrm(ks[8], (DEPTH, B_V, D_MODEL), B_V ** -0.5),
        "b_gate": nrm(ks[9], (DEPTH, 2 * D_MODEL), 0.01),
        "w_out": nrm(ks[10], (DEPTH, D_MODEL, D_MODEL), D_MODEL ** -0.5),
        "g_mix": 1.0 + nrm(ks[11], (DEPTH, D_MODEL), 0.02),
        "g_ffn": 1.0 + nrm(ks[12], (DEPTH, D_MODEL), 0.02),
        "w_query": nrm(ks[13], (DEPTH, D_MODEL, P_HEADS * P_DKEY), D_MODEL ** -0.5),
        "sub_keys_1": nrm(ks[14], (DEPTH, P_NKEYS, P_HALF), P_HALF ** -0.5),
        "sub_keys_2": nrm(ks[15], (DEPTH, P_NKEYS, P_HALF), P_HALF ** -0.5),
        "expert_u": nrm(ks[16], (DEPTH, P_EXPERTS, D_MODEL), D_MODEL ** -0.5),
        "expert_v": nrm(ks[17], (DEPTH, P_EXPERTS, D_MODEL), D_MODEL ** -0.5),
        "g_final": 1.0 + nrm(ks[18], (D_MODEL,), 0.02),
    }


def reference(x_prompt, x_sample, state_ret, w_in, w_s, b_s, g_sgu, w_proj_a, w_proj_b, b_gate,
              w_out, g_mix, g_ffn, w_query, sub_keys_1, sub_keys_2, expert_u, expert_v, g_final):
    log_g = _log_gamma()
    xp, xs = x_prompt, x_sample
    sp_list, ss_list, v_list = [], [], []
    for l in range(DEPTH):
        p = (w_in[l], w_s[l], b_s[l], g_sgu[l], w_proj_a[l], w_proj_b[l], b_gate[l], w_out[l],
             g_mix[l], g_ffn[l], w_query[l], sub_keys_1[l], sub_keys_2[l], expert_u[l], expert_v[l])
        s0p = jnp.zeros((xp.shape[0], B_HEADS, B_DK, B_DV), xp.dtype)
        xp, sp, _ = _layer(xp, 0, s0p, log_g, *p)
        xs, ss, vs = _layer(xs, PAST_LEN, state_ret[l], log_g, *p)
        sp_list.append(sp)
        ss_list.append(ss)
        v_list.append(vs)
    y_prompt = rmsnorm(xp, g_final)
    y_sample = rmsnorm(xs, g_final)
    state_ret_prompt = jnp.stack(sp_list)
    state_ret_sample = jnp.stack(ss_list)
    sgu_v_sample = jnp.stack(v_list)
    return (y_prompt, y_sample, state_ret_prompt, state_ret_sample, sgu_v_sample)
```

```python
from contextlib import ExitStack
import numpy as np
import concourse.bass as bass
import concourse.mybir as mybir
from concourse.bass_utils import run_bass_kernel_spmd

F32 = mybir.dt.float32
BF16 = mybir.dt.bfloat16
AF = mybir.ActivationFunctionType
ALU = mybir.AluOpType

D = 1024
SEQ = 2048
DEC_SEQ = 16
PAST = 1024
NCORES = 8
EPS = 1e-6
NEXP = 16384
IN_COLS = 10240
BORDER = [4, 5, 6, 7] + [0, 1, 2, 3] + list(range(8, 20))
NEG = -1.0e30


class Buf:
    __slots__ = ("name", "t", "lw", "rd", "dsem", "root")

    def __init__(self, name, t=None, root=None):
        self.name = name
        self.t = t
        self.lw = {}
        self.rd = {}
        self.dsem = None
        self.root = root if root is not None else self

    def __getitem__(self, k):
        return self.t[k]

    def view(self, name, ap):
        return Buf(name, ap, root=self.root)


class Ctx:
    def __init__(self, nc, stack):
        self.nc = nc
        self.stack = stack
        self.sems = {}
        self.cnt = {}
        self.engs = {"pe": nc.tensor, "act": nc.scalar, "dve": nc.vector,
                     "pool": nc.gpsimd, "sp": nc.sync}
        self.waited = {e: {} for e in self.engs}
        for e in self.engs:
            self._mksem("E_" + e)
        self.n_inst = 0
        self.n_wait = 0

    def _mksem(self, key):
        s = self.stack.enter_context(self.nc.semaphore(key))
        self.sems[key] = s
        self.cnt[key] = 0
        return s

    def sb(self, name, shape, dt=F32):
        return Buf(name, self.stack.enter_context(self.nc.sbuf_tensor(name, list(shape), dt)))

    def ps(self, name, shape, dt=F32):
        return Buf(name, self.stack.enter_context(self.nc.psum_tensor(name, list(shape), dt)))

    def dram(self, name, shape, dt, kind="Internal"):
        return Buf(name, self.nc.dram_tensor(name, list(shape), dt, kind=kind).ap())

    @staticmethod
    def _merge(dst, src):
        for k, v in src.items():
            if dst.get(k, 0) < v:
                dst[k] = v

    def _needs(self, reads, writes):
        need = {}
        for b in reads:
            self._merge(need, b.root.lw)
        for b in writes:
            self._merge(need, b.root.lw)
            self._merge(need, b.root.rd)
        return need

    def _emit_waits(self, eng, need, skip_self=False):
        e = self.engs[eng]
        w = self.waited[eng]
        for k, v in need.items():
            if skip_self and k == "E_" + eng:
                continue
            if w.get(k, 0) >= v:
                continue
            e.wait_ge(self.sems[k], v)
            w[k] = v
            self.n_wait += 1

    def _retire(self, reads, writes, tk, acc):
        k, v = tk
        for b in reads:
            b = b.root
            if b.rd.get(k, 0) < v:
                b.rd[k] = v
        for b in writes:
            b = b.root
            if acc:
                if b.lw.get(k, 0) < v:
                    b.lw[k] = v
            else:
                b.lw = {k: v}
                b.rd = {}

    def op(self, eng, fn, reads=(), writes=(), skip_self=False):
        need = self._needs(reads, writes)
        self._emit_waits(eng, need, skip_self=skip_self)
        ins = fn(self.engs[eng])
        key = "E_" + eng
        self.cnt[key] += 1
        ins.then_inc(self.sems[key], 1)
        self._retire(reads, writes, (key, self.cnt[key]), False)
        self.n_inst += 1

    def dma(self, eng, out_ap, in_ap, reads=(), writes=(), owner=None, acc=False):
        need = self._needs(reads, writes)
        self._emit_waits(eng, need)
        owner = owner.root
        if owner.dsem is None:
            owner.dsem = "D_" + owner.name
            self._mksem(owner.dsem)
        ins = self.engs[eng].dma_start(out=out_ap, in_=in_ap)
        self.cnt[owner.dsem] += 16
        ins.then_inc(self.sems[owner.dsem], 16)
        self._retire(reads, writes, (owner.dsem, self.cnt[owner.dsem]), acc)
        self.n_inst += 1

    def handoff(self, srcs, dsts):
        for d in dsts:
            d = d.root
            for s_ in srcs:
                s_ = s_.root
                self._merge(d.lw, s_.lw)
                self._merge(d.rd, s_.rd)

    def wait_all(self, eng, bufs):
        self._emit_waits(eng, self._needs(bufs, bufs))


def _gammas():
    return 1.0 - 2.0 ** (-5.0 - np.arange(4, dtype=np.float64))


def _rot_tables(pos):
    half = 128
    inv = 1.0 / (10000.0 ** np.linspace(0.0, 1.0, half, dtype=np.float32))
    ang = pos.astype(np.float32)[None, :] * inv[:, None]
    return np.cos(ang).astype(np.float32), np.sin(ang).astype(np.float32)


def _kind_consts(P):
    g = _gammas()
    idx = np.arange(P, dtype=np.float64)
    diff = idx[None, :] - idx[:, None]
    DT = np.zeros((P, 4, P), np.float32)
    for h in range(4):
        DT[:, h, :] = np.where(diff >= 0, g[h] ** np.maximum(diff, 0), 0.0) * (256 ** -0.5)
    qdec = np.zeros((128, 4, P), np.float32)
    kdec = np.zeros((P, 4), np.float32)
    for h in range(4):
        qdec[:, h, :] = (g[h] ** (idx + 1.0))[None, :]
        kdec[:, h] = g[h] ** (P - 1.0 - idx) * (256 ** -0.5)
    gL = [float(g[h] ** P) for h in range(4)]
    tril = np.tril(np.ones((P, P), np.float32))
    return DT, qdec, kdec, gL, tril


def build_nc(NSEQ=2, NT=16, NSAMP=2):
    rec = []
    _build_nc(NSEQ, NT, NSAMP, rec, None)
    return _build_nc(NSEQ, NT, NSAMP, None, rec)


def _build_nc(NSEQ, NT, NSAMP, wrec, wplan):
    nc = bass.Bass("TRN2", target_bir_lowering=False)
    LP = NT * 128

    def din(name, shape, dt=F32):
        return nc.dram_tensor(name, list(shape), dt, kind="ExternalInput").ap()

    def dout(name, shape, dt=F32):
        return nc.dram_tensor(name, list(shape), dt, kind="ExternalOutput").ap()

    xp = din("xp", [max(NSEQ, 1) * LP, D])
    xsm = din("xsm", [max(NSAMP, 1) * DEC_SEQ, D])
    st_in = din("st_in", [max(NSAMP, 1), 4, 256, 512])
    w_in = din("w_in", [D, IN_COLS])
    w_s = din("w_s", [8, 128, 128])
    b_s = din("b_s", [8, 128])
    g_sgu = din("g_sgu", [1, D])
    w_pa = din("w_pa", [D, D])
    w_pb = din("w_pb", [2 * D, D])
    b_gate = din("b_gate", [1, 2 * D])
    w_out = din("w_out", [D, D])
    g_mix = din("g_mix", [1, D])
    g_ffn = din("g_ffn", [1, D])
    w_q = din("w_q", [D, 2 * D])
    k1 = din("k1", [128, 128])
    k2 = din("k2", [128, 128])
    eu = din("eu", [NEXP, D])
    ev = din("ev", [NEXP, D])
    g_fin = din("g_fin", [1, D])
    cos_p = din("cos_p", [128, SEQ]); sin_p = din("sin_p", [128, SEQ])
    cos_s = din("cos_s", [128, DEC_SEQ]); sin_s = din("sin_s", [128, DEC_SEQ])
    cDT = {128: din("DT_p", [128, 4, 128]), 16: din("DT_s", [16, 4, 16])}
    cQD = {128: din("qd_p", [128, 4, 128]), 16: din("qd_s", [128, 4, 16])}
    cKD = {128: din("kd_p", [128, 4]), 16: din("kd_s", [16, 4])}
    cTR = {128: din("tr_p", [128, 128]), 16: din("tr_s", [16, 16])}

    y_p = dout("y_p", [max(NSEQ, 1) * LP, D])
    y_s = dout("y_s", [max(NSAMP, 1) * DEC_SEQ, D])
    so_p = dout("so_p", [max(NSEQ, 1), 4, 256, 512])
    so_s = dout("so_s", [max(NSAMP, 1), 4, 256, 512])
    sv_s = dout("sv_s", [max(NSAMP, 1) * DEC_SEQ, D])

    gLs = {P: _kind_consts(P)[3] for P in (128, 16)}

    with ExitStack() as st:
        c = Ctx(nc, st)
        win_bf = c.dram("win_bf", [20, 128, 4096], BF16)
        wa_bf = c.dram("wa_bf", [2, 128, 4096], BF16)
        wb_bf = c.dram("wb_bf", [4, 128, 4096], BF16)
        wo_bf = c.dram("wo_bf", [2, 128, 4096], BF16)
        wq_bf = c.dram("wq_bf", [4, 128, 4096], BF16)
        ev_bf = c.dram("ev_bf", [NEXP // 512, 128, 4096], BF16)
        euT_bf = c.dram("euT_bf", [NEXP // 512, 128, 4096], BF16)

        identb = c.sb("identb", [128, 128], BF16)
        identf = c.sb("identf", [128, 128], F32)
        gmix_c = c.sb("gmix_c", [128, 8]); gffn_c = c.sb("gffn_c", [128, 8])
        gfin_bc = c.sb("gfin_bc", [128, D]); gsgu_bc = c.sb("gsgu_bc", [128, D])
        keysT = c.sb("keysT", [128, 2, 128], BF16)
        eps_t = c.sb("eps_t", [128, 1])
        KC = {}
        for P in (128, 16):
            KC[P] = dict(
                DT=c.sb(f"DT{P}", [P, 4, P]), QD=c.sb(f"QD{P}", [128, 4, P]), KD=c.sb(f"KD{P}", [P, 4]),
                wsT=c.sb(f"wsT{P}", [P, 8, P], BF16), bs=c.sb(f"bs{P}", [128, 8, P]))
        cosb = c.sb("cosb", [128, 128]); sinb = c.sb("sinb", [128, 128])
        S = c.sb("S", [128, 4, 2, 512])
        W = [c.sb(f"W{i}", [128, 8, 512], BF16) for i in range(3)]
        V = [c.sb(f"V{i}", [128, 4, 1024], BF16) for i in range(2)]
        xsb = [c.sb(f"xs{i}", [128, D]) for i in range(2)]
        hb = c.sb("hb", [128, D], BF16); hTm = c.sb("hTm", [128, 8, 128], BF16)
        hTpb = [c.sb(f"hTp{i}", [128, 8, 128], BF16) for i in range(2)]
        uT = c.sb("uT", [128, 8, 128], BF16); vg = c.sb("vg", [128, D]); vn = c.sb("vn", [128, D], BF16)
        qT = c.sb("qT", [128, 8, 128], BF16); qdT = c.sb("qdT", [128, 8, 128], F32)
        kT = c.sb("kT", [128, 8, 128], BF16); kdk = c.sb("kdk", [128, 4, 256], BF16)
        rq = c.sb("rq", [128, 4, 128]); rt1 = c.sb("rt1", [128, 2, 128]); rt2 = c.sb("rt2", [128, 2, 128])
        CT = c.sb("CT", [128, 16384], BF16)
        VSG = c.sb("VSG", [128, 8192], BF16)
        vr = VSG.view("vr", VSG[:, 0:2048]); sg = VSG.view("sg", VSG[:, 2048:4096])
        gs = VSG.view("gs", VSG[:, 4096:6144]); yb = VSG.view("yb", VSG[:, 6144:8192])
        bgrow = c.sb("bgrow", [1, 2 * D], BF16); ones_b = c.sb("ones_b", [1, 128], BF16)
        yaT = c.sb("yaT", [128, 8, 128], BF16)
        ybT = c.sb("ybT", [128, 16, 128], BF16)
        scT = c.sb("scT", [128, 4, 128], BF16)
        mb = vn; mT = c.sb("mT", [128, 8, 128], BF16)
        t512 = c.sb("t512", [128, 512]); t512b = rq.view("t512b", rq[:].rearrange("p a b -> p (a b)"))
        st1 = c.sb("st1", [128, 8]); st2 = c.sb("st2", [128, 8]); st3 = c.sb("st3", [128, 8])
        pq = c.sb("pq", [128, 16, 128], BF16)
        s_all = VSG.view("s_all", VSG[:, 4096:8192].bitcast(F32).rearrange("p (c n) -> p c n", n=128))
        cand = VSG.view("cand", VSG[:, 0:4096].bitcast(F32))
        tk = c.sb("tk", [128, 16, 24]); tmpm = c.sb("tmpm", [128, 256]); tmpm2 = c.sb("tmpm2", [128, 256])
        ctop = c.sb("ctop", [128, 8, 24])
        thr = c.sb("thr", [128, 8]); negc = c.sb("negc", [128, 8]); ncm = c.sb("ncm", [128, 8])
        Zs = c.sb("Zs", [128, 8]); lnZ = c.sb("lnZ", [128, 8]); junk16 = c.sb("junk16", [128, 16])
        RC = c.sb("RC", [128, 4, 128]); FMC = c.sb("FMC", [128, 4, 128])
        NTS = 4
        TB = 4
        rep8 = [c.sb(f"rep8_{i}", [128, TB, 256], BF16) for i in range(2)]
        Eb = [c.sb(f"Eb{i}", [128, 256]) for i in range(NTS)]
        Xb = [c.sb(f"Xb{i}", [128, 128]) for i in range(NTS)]
        Lb = [c.sb(f"Lb{i}", [128, 128], BF16) for i in range(NTS)]
        Rb = [c.sb(f"Rb{i}", [128, 128], BF16) for i in range(NTS)]
        gA = [c.sb("gA0", [128, 4, 128])] * 2
        CAT = [c.sb(f"CAT{i}", [128, 4, 128], BF16) for i in range(2)]
        yo = vg
        pT = [c.ps(f"pT{i}", [128, 8, 128], BF16) for i in range(2)]
        B = [c.ps(f"B{i}", [128, 512], F32) for i in range(6)]

        def b4(bank):
            return bank[:].rearrange("p (c n) -> p c n", n=128)

        def ld(eng, dst, src):
            c.dma(eng, dst[:], src, writes=[dst], owner=dst)

        ld("sp", gfin_bc, g_fin.to_broadcast([128, D])); ld("sp", gsgu_bc, g_sgu.to_broadcast([128, D]))
        c.op("pool", lambda e: e.memset(ones_b[:], 1.0), writes=[ones_b])
        c.dma("pool", bgrow[0:1, :], b_gate[:, :], writes=[bgrow], owner=bgrow)
        c.op("pool", lambda e: e.memset(eps_t[:], EPS), writes=[eps_t])
        c.op("pool", lambda e: e.memset(identf[:], 0.0), writes=[identf])
        c.op("pool", lambda e: e.affine_select(out=identf[:], in_=identf[:], pattern=[[-1, 128]],
                                               compare_op=ALU.not_equal, fill=1.0, base=0, channel_multiplier=1),
             reads=[identf], writes=[identf])
        c.op("dve", lambda e: e.tensor_copy(out=identb[:], in_=identf[:]), reads=[identf], writes=[identb])
        for gsrc, gdst in ((g_mix, gmix_c), (g_ffn, gffn_c)):
            c.dma("sp", t512[:8, 0:128], gsrc[0, :].rearrange("(k p) -> k p", p=128), writes=[t512], owner=t512)
            c.op("pe", lambda e: e.transpose(out=B[0][:, 0:8], in_=t512[:8, 0:128], identity=identf[:8, :8]),
                 reads=[t512, identf], writes=[B[0]], skip_self=True)
            c.op("dve", lambda e: e.tensor_copy(out=gdst[:, :], in_=B[0][:, 0:8]), reads=[B[0]], writes=[gdst])
        for hi, kk in enumerate((k1, k2)):
            c.dma("sp", t512[:, 0:128], kk[:, :], writes=[t512], owner=t512)
            c.op("pe", lambda e: e.transpose(out=B[0][:, 0:128], in_=t512[:, 0:128], identity=identf[:]),
                 reads=[t512, identf], writes=[B[0]], skip_self=True)
            c.op("dve", lambda e: e.tensor_copy(out=keysT[:, hi, :], in_=B[0][:, 0:128]), reads=[B[0]], writes=[keysT])
        for P in (128, 16):
            kc = KC[P]
            ld("sp", kc["DT"], cDT[P][:, :, :]); ld("sp", kc["QD"], cQD[P][:, :, :]); ld("sp", kc["KD"], cKD[P][:, :])
            c.dma("sp", kc["bs"][:], b_s[:, 0:P].unsqueeze(0).to_broadcast([128, 8, P]), writes=[kc["bs"]], owner=kc["bs"])
            c.dma("sp", t512b[:P, 0:P], cTR[P][:, :], writes=[t512b], owner=t512b)
            for g in range(8):
                c.dma("sp", t512[:P, 0:P], w_s[g, 0:P, 0:P], writes=[t512], owner=t512)
                c.op("dve", lambda e: e.tensor_tensor(out=t512[:P, 128:128 + P], in0=t512[:P, 0:P], in1=t512b[:P, 0:P], op=ALU.mult),
                     reads=[t512, t512b], writes=[t512])
                c.op("pe", lambda e: e.transpose(out=B[0][:P, 0:P], in_=t512[:P, 128:128 + P], identity=identf[:P, :P]),
                     reads=[t512, identf], writes=[B[0]], skip_self=True)
                c.op("dve", lambda e: e.tensor_copy(out=kc["wsT"][:P, g, :], in_=B[0][:P, 0:P]), reads=[B[0]], writes=[kc["wsT"]])

        def cast_w(dst, bi, src, r0, c0):
            c.dma("pool", dst[bi].rearrange("p (k c) -> p k c", c=512),
                  src[r0:r0 + D, c0:c0 + 512].rearrange("(k p) c -> p k c", p=128), writes=[dst], owner=dst, acc=True)
        for j in range(20):
            cast_w(win_bf, j, w_in, 0, j * 512)
        for nb in range(2):
            cast_w(wa_bf, nb, w_pa, 0, nb * 512)
            cast_w(wb_bf, nb * 2, w_pb, 0, nb * 512); cast_w(wb_bf, nb * 2 + 1, w_pb, D, nb * 512)
            cast_w(wo_bf, nb, w_out, 0, nb * 512)
        for j in range(4):
            cast_w(wq_bf, j, w_q, 0, j * 512)
        for eb in range(NEXP // 512):
            c.dma("pool", ev_bf[eb].rearrange("p (c d) -> p c d", d=D),
                  ev[eb * 512:(eb + 1) * 512, :].rearrange("(c p) d -> p c d", p=128), writes=[ev_bf], owner=ev_bf, acc=True)
        stg = [Buf("stgA", CT[:, 0:4096].bitcast(F32)), Buf("stgB", CT[:, 8192:12288].bitcast(F32))]
        for eb in range(NEXP // 512):
            for hf in range(2):
                c.dma("sp", stg[hf][:, :].rearrange("p (c d) -> p c d", d=D),
                      eu[eb * 512 + hf * 256: eb * 512 + (hf + 1) * 256, :].rearrange("(c p) d -> p c d", p=128),
                      writes=[stg[hf]], owner=stg[hf])
            wsl = W[eb % 2]
            for k in range(8):
                bank = B[k % 4]
                for ec in range(4):
                    src = stg[ec // 2][:, (ec % 2) * D + k * 128:(ec % 2) * D + (k + 1) * 128]
                    c.op("pe", lambda e: e.transpose(out=bank[:, ec * 128:(ec + 1) * 128], in_=src, identity=identf[:]),
                         reads=[stg[ec // 2], identf], writes=[bank], skip_self=True)
                eng = "act" if k % 2 == 0 else "dve"
                if eng == "act":
                    c.op("act", lambda e: e.copy(out=wsl[:, k, :], in_=bank[:]), reads=[bank], writes=[wsl])
                else:
                    c.op("dve", lambda e: e.tensor_copy(out=wsl[:, k, :], in_=bank[:]), reads=[bank], writes=[wsl])
            c.dma("sp", euT_bf[eb], wsl[:].rearrange("p k e -> p (k e)"),
                  reads=[wsl], writes=[euT_bf], owner=wsl, acc=True)

        c.handoff(stg, [CT])
        ps_s = [B[0], B[1], B[2]]
        pc_s = [B[3], B[4], B[5]]
        pTf = [pT[i].view(f"pTf{i}", pT[i][:].rearrange("p a b -> p (a b)").bitcast(F32)) for i in range(2)]

        class Stream:
            def __init__(self, slots):
                self.slots = slots
                self.plan = []
                self.emitted = 0

            def add(self, out_fn, src_ap, src_buf):
                self.plan.append((out_fn, src_ap, src_buf))
                return len(self.plan) - 1

            def use(self, i):
                while self.emitted <= min(i + len(self.slots) - 1, len(self.plan) - 1):
                    j = self.emitted
                    out_fn, src_ap, src_buf = self.plan[j]
                    sl = self.slots[j % len(self.slots)]
                    c.dma("sp", out_fn(sl), src_ap, reads=[src_buf], writes=[sl], owner=sl)
                    self.emitted += 1
                return self.slots[i % len(self.slots)]

        WS = Stream(W)
        VS = Stream(V)

        def plan_w(scr, bi):
            return WS.add(lambda sl: sl[:].rearrange("p k c -> p (k c)"), scr[bi], scr)

        def transposes_to(dst, src, nchunk, P, rd):
            for c0 in range(0, nchunk, 8):
                pt = pT[(c0 // 8) % 2]
                for k in range(8):
                    c.op("pe", lambda e: e.transpose(out=pt[:, k, :P], in_=src[:P, (c0 + k) * 128:(c0 + k + 1) * 128],
                                                     identity=identb[:P, :P]),
                         reads=[rd, identb], writes=[pt], skip_self=True)
                c.op("act", lambda e: e.copy(out=dst[:, c0:c0 + 8, :P], in_=pt[:, :, :P]), reads=[pt], writes=[dst])

        def transposes_gen(dst, src, nchunk, P, rd):
            for c0 in range(0, nchunk, 8):
                pt = pT[(c0 // 8) % 2]
                for k in range(8):
                    c.op("pe", lambda e: e.transpose(out=pt[:, k, :P], in_=src[:P, (c0 + k) * 128:(c0 + k + 1) * 128],
                                                     identity=identb[:P, :P]),
                         reads=[rd, identb], writes=[pt], skip_self=True)
                yield
                c.op("act", lambda e: e.copy(out=dst[:, c0:c0 + 8, :P], in_=pt[:, :, :P]), reads=[pt], writes=[dst])

        def rstd_chain(ssq, nw, P, inv_n, out, c0=0):
            c.op("act", lambda e: e.activation(out=st2[:P, c0:c0 + nw], in_=ssq[:P, c0:c0 + nw], func=AF.Sqrt, bias=eps_t[:P, :], scale=inv_n),
                 reads=[ssq, eps_t], writes=[st2])
            c.op("dve", lambda e: e.reciprocal(out=out[:P, c0:c0 + nw], in_=st2[:P, c0:c0 + nw]), reads=[st2], writes=[out])

        def rmsnorm_to_hT(P, gbc, xs, hT):
            c.op("act", lambda e: e.activation(out=hb[:P, :], in_=xs[:P, :], func=AF.Square, accum_out=st1[:P, 0:1]),
                 reads=[xs], writes=[hb, st1])
            rstd_chain(st1, 1, P, 1.0 / D, st3)
            c.op("dve", lambda e: e.tensor_scalar(out=hb[:P, :], in0=xs[:P, :], scalar1=st3[:P, 0:1], scalar2=None, op0=ALU.mult),
                 reads=[xs, st3], writes=[hb])
            pt = pT[0]
            for k in range(8):
                c.op("pe", lambda e: e.transpose(out=pt[:, k, :P], in_=hb[:P, k * 128:(k + 1) * 128], identity=identb[:P, :P]),
                     reads=[hb, identb], writes=[pt], skip_self=True)
            c.op("dve", lambda e: e.tensor_tensor(out=hT[:, :, :P], in0=pt[:, :, :P], in1=gbc[:, :].unsqueeze(2).to_broadcast([128, 8, P]), op=ALU.mult),
                 reads=[pt, gbc], writes=[hT])

        bank_rr = [0]

        def next_bank(lo=0, hi=3):
            b = B[lo + bank_rr[0] % (hi - lo)]
            bank_rr[0] += 1
            return b

        def mm_tm(bank, P, lhs, lhs_buf, nk, wsl, first=True, last=True, k0=0):
            for k in range(nk):
                c.op("pe", lambda e: e.matmul(bank[:P, :], lhsT=lhs[:, k0 + k, :P], rhs=wsl[:, k, :],
                                              start=(first and k == 0), stop=(last and k == nk - 1)),
                     reads=[lhs_buf, wsl], writes=[bank], skip_self=True)

        def mm_fm(bank, P, rhs, rhs_buf, wsl):
            bv = b4(bank)
            for cc in range(4):
                for k in range(8):
                    c.op("pe", lambda e: e.matmul(bv[:, cc, :P], lhsT=wsl[:, k, cc * 128:(cc + 1) * 128], rhs=rhs[:, k, :P],
                                                  start=(k == 0), stop=(k == 7)),
                         reads=[rhs_buf, wsl], writes=[bank], skip_self=True)
            return bv

        def mm_tm_gen(bank, P, lhs, lhs_buf, nk, wsl, first=True, last=True, k0=0):
            for k in range(nk):
                c.op("pe", lambda e: e.matmul(bank[:P, :], lhsT=lhs[:, k0 + k, :P], rhs=wsl[:, k, :],
                                              start=(first and k == 0), stop=(last and k == nk - 1)),
                     reads=[lhs_buf, wsl], writes=[bank], skip_self=True)
                if k % 4 == 3:
                    yield

        def mm_fm_gen(bank, P, rhs, rhs_buf, wsl):
            bv = b4(bank)
            for cc in range(4):
                for k in range(8):
                    c.op("pe", lambda e: e.matmul(bv[:, cc, :P], lhsT=wsl[:, k, cc * 128:(cc + 1) * 128], rhs=rhs[:, k, :P],
                                                  start=(k == 0), stop=(k == 7)),
                         reads=[rhs_buf, wsl], writes=[bank], skip_self=True)
                yield

        wseq = []
        wpos = [0]

        def nextw(scr, bi):
            if wrec is not None:
                wrec.append((scr.name, bi))
                wpos[0] += 1
                return W[wpos[0] % 3]
            sl = WS.use(wseq[wpos[0]])
            wpos[0] += 1
            return sl

        def kick_w():
            if wrec is None and wpos[0] < len(wseq):
                WS.use(wseq[wpos[0]])

        def mixer_gen(J, part):
            P = J["P"]; xs = J["xs"]; sv_dst = J["sv_dst"]; col0 = J["col0"]; hT = hTm
            kc = KC[P]
            gL = gLs[P]
            if part == "chain":
                yield from mixer_chain(J, P, xs, kc, gL)
                return
            if J["s_init"] == "zero":
                c.op("pool", lambda e: e.memset(S[:], 0.0), writes=[S])
            elif J["s_init"] is not None:
                c.dma("sp", S[:], J["s_init"], writes=[S], owner=S)
            c.dma("sp", xs[:P, :], J["x_src"], writes=[xs], owner=xs)
            c.dma("sp", cosb[:, :P], J["cs_src"][:, col0:col0 + P], writes=[cosb], owner=cosb)
            c.dma("sp", sinb[:, :P], J["sn_src"][:, col0:col0 + P], writes=[sinb], owner=sinb)
            for _ in range(8):
                yield
            rmsnorm_to_hT(P, gmix_c, xs, hTm)
            yield
            mb_rr = [0]

            def pe_part(j, out):
                wsl = nextw(win_bf, j)
                bank = pTf[mb_rr[0] % 2]
                mb_rr[0] += 1
                out.append(bank)
                if j < 2 or 4 <= j < 8:
                    yield from mm_fm_gen(bank, P, hT, hT, wsl)
                else:
                    isgate = j >= 16
                    if isgate:
                        c.op("pe", lambda e: e.matmul(bank[:P, :], lhsT=ones_b[0:1, :P], rhs=bgrow[0:1, (j - 16) * 512:(j - 15) * 512],
                                                      start=True, stop=False),
                             reads=[ones_b, bgrow], writes=[bank], skip_self=True)
                    yield from mm_tm_gen(bank, P, hT, hT, 8, wsl, first=not isgate)

            def post_part(j, bank):
                if j < 2 or 4 <= j < 8:
                    bv = b4(bank)
                    if j < 2:
                        c.op("act", lambda e: e.activation(out=uT[:, j * 4:(j + 1) * 4, :P], in_=bv[:, :, :P], func=AF.Gelu_apprx_tanh),
                             reads=[bank], writes=[uT])
                    else:
                        isq = j < 6
                        jj = (j - 4) % 2
                        x1 = bv[:, 0:4:2, :P]; x2 = bv[:, 1:4:2, :P]
                        cb = cosb[:, :P].unsqueeze(1).to_broadcast([128, 2, P])
                        sb_ = sinb[:, :P].unsqueeze(1).to_broadcast([128, 2, P])
                        dst = qT if isq else kT
                        o1 = rq[:, 0:4:2, :P] if isq else dst[:, jj * 4:(jj + 1) * 4:2, :P]
                        o2 = rq[:, 1:4:2, :P] if isq else dst[:, jj * 4 + 1:(jj + 1) * 4:2, :P]
                        wr = [rq] if isq else [dst]
                        c.op("dve", lambda e: e.tensor_tensor(out=rt1[:, :, :P], in0=x1, in1=cb, op=ALU.mult), reads=[bank, cosb], writes=[rt1])
                        c.op("dve", lambda e: e.tensor_tensor(out=rt2[:, :, :P], in0=x2, in1=sb_, op=ALU.mult), reads=[bank, sinb], writes=[rt2])
                        c.op("dve", lambda e: e.tensor_tensor(out=o1, in0=rt1[:, :, :P], in1=rt2[:, :, :P], op=ALU.subtract),
                             reads=[rt1, rt2], writes=wr)
                        c.op("dve", lambda e: e.tensor_tensor(out=rt1[:, :, :P], in0=x1, in1=sb_, op=ALU.mult), reads=[bank, sinb], writes=[rt1])
                        c.op("dve", lambda e: e.tensor_tensor(out=rt2[:, :, :P], in0=x2, in1=cb, op=ALU.mult), reads=[bank, cosb], writes=[rt2])
                        c.op("dve", lambda e: e.tensor_tensor(out=o2, in0=rt1[:, :, :P], in1=rt2[:, :, :P], op=ALU.add),
                             reads=[rt1, rt2], writes=wr)
                        if isq:
                            c.op("act", lambda e: e.copy(out=qT[:, jj * 4:(jj + 1) * 4, :P], in_=rq[:, :, :P]), reads=[rq], writes=[qT])
                            qd = kc["QD"][:, jj * 2:(jj + 1) * 2, :P].unsqueeze(2).to_broadcast([128, 2, 2, P])
                            c.op("pool", lambda e: e.tensor_tensor(out=qdT[:, jj * 4:(jj + 1) * 4, :P].rearrange("p (h t) n -> p h t n", t=2),
                                                                  in0=rq[:, :, :P].rearrange("p (h t) n -> p h t n", t=2), in1=qd, op=ALU.mult),
                                 reads=[rq, kc["QD"]], writes=[qdT])
                else:
                    if j < 4:
                        c.op("act", lambda e: e.activation(out=vg[:P, (j - 2) * 512:(j - 1) * 512], in_=bank[:P, :], func=AF.Gelu_apprx_tanh),
                             reads=[bank], writes=[vg])
                    elif j < 12:
                        c.op("act", lambda e: e.copy(out=vr[:P, (j - 8) * 512:(j - 7) * 512], in_=bank[:P, :]), reads=[bank], writes=[vr])
                    elif j < 16:
                        c.op("act", lambda e: e.activation(out=sg[:P, (j - 12) * 512:(j - 11) * 512], in_=bank[:P, :], func=AF.Silu),
                             reads=[bank], writes=[sg])
                    else:
                        c.op("act", lambda e: e.activation(out=gs[:P, (j - 16) * 512:(j - 15) * 512], in_=bank[:P, :], func=AF.Sigmoid),
                             reads=[bank], writes=[gs])
                if j == 3:
                    c.op("act", lambda e: e.activation(out=vn[:P, :], in_=vg[:P, :], func=AF.Square, accum_out=st1[:P, 0:1]),
                         reads=[vg], writes=[vn, st1])
                    rstd_chain(st1, 1, P, 1.0 / D, st3)
                    c.op("dve", lambda e: e.scalar_tensor_tensor(out=vn[:P, :], in0=vg[:P, :], scalar=st3[:P, 0:1], in1=gsgu_bc[:P, :],
                                                                 op0=ALU.mult, op1=ALU.mult),
                         reads=[vg, st3, gsgu_bc], writes=[vn])
                    if sv_dst is not None:
                        c.op("dve", lambda e: e.scalar_tensor_tensor(out=vg[:P, :], in0=vg[:P, :], scalar=st3[:P, 0:1], in1=gsgu_bc[:P, :],
                                                                     op0=ALU.mult, op1=ALU.mult),
                             reads=[vg, st3, gsgu_bc], writes=[vg])
                        c.dma("sp", sv_dst, vg[:P, :], reads=[vg], owner=vg)

            def blocks_gen(js):
                prev = None
                for j in js:
                    ob = []
                    yield from pe_part(j, ob)
                    if prev is not None:
                        post_part(*prev)
                    yield
                    prev = (j, ob[0])
                post_part(*prev)
                yield

            for _ in blocks_gen(BORDER):
                yield
            return

        def mixer_chain(J, P, xs, kc, gL):
            if True:
                pass
            for hf in range(2):
                yield
                bank = B[4 + hf]
                bv = b4(bank)
                for g4 in range(4):
                    g = hf * 4 + g4
                    c.op("pe", lambda e: e.matmul(bv[:, g4, :P], lhsT=vn[:P, g * 128:(g + 1) * 128], rhs=kc["wsT"][:P, g, :P],
                                                  start=True, stop=True),
                         reads=[vn, kc["wsT"]], writes=[bank], skip_self=True)
                yield
                c.op("dve", lambda e: e.tensor_tensor(out=rq[:, :, :P], in0=bv[:, :, :P], in1=kc["bs"][:, hf * 4:(hf + 1) * 4, :P], op=ALU.add),
                     reads=[bank, kc["bs"]], writes=[rq])
                c.op("dve", lambda e: e.tensor_tensor(out=yaT[:, hf * 4:(hf + 1) * 4, :P], in0=rq[:, :, :P], in1=uT[:, hf * 4:(hf + 1) * 4, :P],
                                                      op=ALU.mult),
                     reads=[rq, uT], writes=[yaT])
            yield
            bsc = B[5]
            bscv = b4(bsc)
            for h in range(4):
                for cc in range(2):
                    c.op("pe", lambda e: e.matmul(bscv[:P, h, :P], lhsT=kT[:, 2 * h + cc, :P], rhs=qT[:, 2 * h + cc, :P],
                                                  start=(cc == 0), stop=(cc == 1)),
                         reads=[kT, qT], writes=[bsc], skip_self=True)
            pt = pT[1]
            for k in range(8):
                c.op("pe", lambda e: e.transpose(out=pt[:P, k, :], in_=kT[:, k, :P], identity=identb[:, :]),
                     reads=[kT, identb], writes=[pt], skip_self=True)
            yield
            c.op("dve", lambda e: e.tensor_tensor(out=scT[:P, :, :P], in0=bscv[:P, :, :P], in1=kc["DT"][:P, :, :P], op=ALU.mult),
                 reads=[bsc, kc["DT"]], writes=[scT])
            for h in range(4):
                c.op("dve", lambda e: e.tensor_scalar(out=kdk[:P, h, :].rearrange("p (t d) -> p t d", t=2), in0=pt[:P, 2 * h:2 * h + 2, :],
                                                      scalar1=kc["KD"][:P, h:h + 1], scalar2=None, op0=ALU.mult),
                     reads=[pt, kc["KD"]], writes=[kdk])
            for hp in range(2):
                yield
                for h in (2 * hp, 2 * hp + 1):
                    bo = B[4 + h % 2]
                    c.op("pe", lambda e: e.matmul(bo[:P, :], lhsT=scT[:P, h, :P], rhs=vr[:P, h * 512:(h + 1) * 512], start=True, stop=False),
                         reads=[scT, vr], writes=[bo], skip_self=True)
                    for cc in range(2):
                        c.op("pe", lambda e: e.matmul(bo[:P, :], lhsT=qdT[:, 2 * h + cc, :P], rhs=S[:, h, cc, :], start=False, stop=(cc == 1)),
                             reads=[qdT, S], writes=[bo], skip_self=True)
                yield
                for h in (2 * hp, 2 * hp + 1):
                    bo = B[4 + h % 2]
                    c.op("act", lambda e: e.activation(out=yb[:P, h * 512:(h + 1) * 512], in_=bo[:P, :], func=AF.Square, accum_out=st1[:P, h:h + 1]),
                         reads=[bo], writes=[yb, st1])
                yield
                rstd_chain(st1, 2, P, 1.0 / 512, st3, c0=2 * hp)
                yield
                for h in (2 * hp, 2 * hp + 1):
                    bo = B[4 + h % 2]
                    c.op("dve", lambda e: e.scalar_tensor_tensor(out=yb[:P, h * 512:(h + 1) * 512], in0=bo[:P, :], scalar=st3[:P, h:h + 1],
                                                                 in1=sg[:P, h * 512:(h + 1) * 512], op0=ALU.mult, op1=ALU.mult),
                         reads=[bo, st3, sg], writes=[yb])
            yield
            for _ in transposes_gen(ybT, yb, 16, P, yb):
                yield
            for h in range(4):
                yield
                for cc in range(2):
                    bu = B[4 + cc]
                    c.op("pe", lambda e: e.matmul(bu[:, :], lhsT=kdk[:P, h, cc * 128:(cc + 1) * 128], rhs=vr[:P, h * 512:(h + 1) * 512],
                                                  start=True, stop=True),
                         reads=[kdk, vr], writes=[bu], skip_self=True)
                yield
                for cc in range(2):
                    bu = B[4 + cc]
                    c.op("dve", lambda e: e.scalar_tensor_tensor(out=S[:, h, cc, :], in0=S[:, h, cc, :], scalar=gL[h], in1=bu[:, :],
                                                                 op0=ALU.mult, op1=ALU.add),
                         reads=[S, bu], writes=[S])
            if J["s_store"] is not None:
                c.dma("sp", J["s_store"], S[:], reads=[S], owner=S)
            for nb in range(2):
                yield
                ba = B[4]; bb = B[5]
                wsl = nextw(wa_bf, nb)
                mm_tm(ba, P, yaT, yaT, 8, wsl)
                for kh in range(2):
                    wsl = nextw(wb_bf, nb * 2 + kh)
                    mm_tm(bb, P, ybT, ybT, 8, wsl, first=(kh == 0), last=(kh == 1), k0=kh * 8)
                    yield
                c.op("dve", lambda e: e.tensor_tensor(out=t512[:P, :], in0=ba[:P, :], in1=gs[:P, nb * 512:(nb + 1) * 512], op=ALU.mult),
                     reads=[ba, gs], writes=[t512])
                c.op("dve", lambda e: e.tensor_tensor(out=t512b[:P, :], in0=bb[:P, :], in1=gs[:P, D + nb * 512:D + (nb + 1) * 512], op=ALU.mult),
                     reads=[bb, gs], writes=[t512b])
                c.op("pool", lambda e: e.tensor_tensor(out=mb[:P, nb * 512:(nb + 1) * 512], in0=t512[:P, :], in1=t512b[:P, :], op=ALU.add),
                     reads=[t512, t512b], writes=[mb])
            yield
            yield from transposes_gen(mT, mb, 8, P, mb)
            for nb in range(2):
                yield
                bx = B[4 + nb]
                wsl = nextw(wo_bf, nb)
                mm_tm(bx, P, mT, mT, 8, wsl)
            for nb in range(2):
                yield
                bx = B[4 + nb]
                c.op("dve", lambda e: e.tensor_tensor(out=xs[:P, nb * 512:(nb + 1) * 512], in0=bx[:P, :], in1=xs[:P, nb * 512:(nb + 1) * 512], op=ALU.add),
                     reads=[bx, xs], writes=[xs])

        def routing_gen(J):
            P = J["P"]; xs = J["xs"]; hT = J["hTp"]
            rmsnorm_to_hT(P, gffn_c, xs, hT)
            yield
            for j in range(4):
                if j:
                    yield
                wsl = nextw(wq_bf, j)
                bank = next_bank(4, 6)
                bv = mm_fm(bank, P, hT, hT, wsl)
                c.op("act", lambda e: e.copy(out=pq[:].rearrange("p (t2 h) n -> p h t2 n", t2=2)[:, 2 * j:2 * j + 2, :, :P],
                                            in_=bv[:, :, :P].rearrange("p (h t2) n -> p h t2 n", t2=2)), reads=[bank], writes=[pq])
            for q4 in range(4):
                yield
                bank = B[4 + q4 % 2]
                bv = b4(bank)
                for i4 in range(4):
                    c16 = q4 * 4 + i4
                    c.op("pe", lambda e: e.matmul(bv[:P, i4, :], lhsT=pq[:, (c16 % 2) * 8 + c16 // 2, :P], rhs=keysT[:, c16 % 2, :], start=True, stop=True),
                         reads=[pq, keysT], writes=[bank], skip_self=True)
                c.op("act", lambda e: e.copy(out=s_all[:P, q4 * 4:(q4 + 1) * 4, :], in_=bv[:P, :, :]), reads=[bank], writes=[s_all])
            for c16 in range(16):
                yield
                c.op("dve", lambda e: e.max(out=tk[:P, c16, 0:8], in_=s_all[:P, c16, :]), reads=[s_all], writes=[tk])
                c.op("dve", lambda e: e.match_replace(out=tmpm[:P, 0:128], in_to_replace=tk[:P, c16, 0:8], in_values=s_all[:P, c16, :], imm_value=NEG),
                     reads=[s_all, tk], writes=[tmpm])
                c.op("dve", lambda e: e.max(out=tk[:P, c16, 8:16], in_=tmpm[:P, 0:128]), reads=[tmpm], writes=[tk])
                if c16 % 2 == 0:
                    c.op("dve", lambda e: e.match_replace(out=tmpm2[:P, 0:128], in_to_replace=tk[:P, c16, 8:16], in_values=tmpm[:P, 0:128], imm_value=NEG),
                         reads=[tmpm, tk], writes=[tmpm2])
                    c.op("dve", lambda e: e.max(out=tk[:P, c16, 16:24], in_=tmpm2[:P, 0:128]), reads=[tmpm2], writes=[tk])
            yield
            tk4 = tk[:].rearrange("p (h t) a -> p h t a", t=2)
            c.op("pool", lambda e: e.tensor_tensor(out=cand[:P, :].rearrange("p (h a b) -> p h a b", a=16, b=16),
                                                  in0=tk4[:P, :, 0, 0:16].unsqueeze(3).to_broadcast([P, 8, 16, 16]),
                                                  in1=tk4[:P, :, 1, 0:16].unsqueeze(2).to_broadcast([P, 8, 16, 16]), op=ALU.add),
                 reads=[tk], writes=[cand])
            for h in range(8):
                yield
                c.op("dve", lambda e: e.max(out=ctop[:P, h, 0:8], in_=cand[:P, h * 256:(h + 1) * 256]), reads=[cand], writes=[ctop])
                c.op("dve", lambda e: e.match_replace(out=tmpm[:P, :], in_to_replace=ctop[:P, h, 0:8], in_values=cand[:P, h * 256:(h + 1) * 256], imm_value=NEG),
                     reads=[cand, ctop], writes=[tmpm])
                c.op("dve", lambda e: e.max(out=ctop[:P, h, 8:16], in_=tmpm[:P, :]), reads=[tmpm], writes=[ctop])
                c.op("dve", lambda e: e.match_replace(out=tmpm2[:P, :], in_to_replace=ctop[:P, h, 8:16], in_values=tmpm[:P, :], imm_value=NEG),
                     reads=[tmpm, ctop], writes=[tmpm2])
                c.op("dve", lambda e: e.max(out=ctop[:P, h, 16:24], in_=tmpm2[:P, :]), reads=[tmpm2], writes=[ctop])
            yield
            c.op("dve", lambda e: e.tensor_tensor(out=thr[:P, :], in0=ctop[:P, :, 15], in1=ctop[:P, :, 16], op=ALU.add), reads=[ctop], writes=[thr])
            c.op("dve", lambda e: e.tensor_scalar(out=thr[:P, :], in0=thr[:P, :], scalar1=0.5, scalar2=None, op0=ALU.mult), reads=[thr], writes=[thr])
            c.op("dve", lambda e: e.tensor_scalar(out=ncm[:P, :], in0=ctop[:P, :, 0], scalar1=-1.0, scalar2=None, op0=ALU.mult), reads=[ctop], writes=[ncm])
            for h in range(8):
                c.op("act", lambda e: e.activation(out=junk16[:P, :], in_=ctop[:P, h, 0:16], func=AF.Exp, bias=ncm[:P, h:h + 1], accum_out=Zs[:P, h:h + 1]),
                     reads=[ctop, ncm], writes=[junk16, Zs])
            yield
            c.op("act", lambda e: e.activation(out=lnZ[:P, :], in_=Zs[:P, :], func=AF.Ln), reads=[Zs], writes=[lnZ])
            yield
            rcv = lambda r: RC[:P, r, :].rearrange("p (h a) -> p h a", a=16)
            c.op("dve", lambda e: e.tensor_tensor(out=rcv(0), in0=tk4[:P, :, 0, 0:16], in1=tk4[:P, :, 0, 1:17], op=ALU.add), reads=[tk], writes=[RC])
            c.op("dve", lambda e: e.tensor_scalar(out=RC[:P, 0, :], in0=RC[:P, 0, :], scalar1=0.5, scalar2=None, op0=ALU.mult), reads=[RC], writes=[RC])
            c.op("dve", lambda e: e.tensor_tensor(out=rcv(1), in0=thr[:P, :].unsqueeze(2).to_broadcast([P, 8, 16]), in1=tk4[:P, :, 0, 0:16], op=ALU.subtract),
                 reads=[thr, tk], writes=[RC])
            c.op("dve", lambda e: e.tensor_copy(out=rcv(2)[:, :, 0:15], in_=rcv(1)[:, :, 1:16]), reads=[RC], writes=[RC])
            c.op("dve", lambda e: e.memset(rcv(2)[:, :, 15:16], 1.0e30), reads=[RC], writes=[RC])
            c.op("dve", lambda e: e.tensor_tensor(out=negc[:P, :], in0=ncm[:P, :], in1=lnZ[:P, :], op=ALU.subtract), reads=[ncm, lnZ], writes=[negc])
            c.op("dve", lambda e: e.tensor_scalar(out=rcv(3), in0=negc[:P, :].unsqueeze(2).to_broadcast([P, 8, 16]), scalar1=0.5, scalar2=None, op0=ALU.mult),
                 reads=[negc], writes=[RC])
            yield
            for r in range(4):
                c.op("pe", lambda e: e.transpose(out=b4(B[4])[:, r, :P], in_=RC[:P, r, :], identity=identf[:P, :P]),
                     reads=[RC, identf], writes=[B[4]], skip_self=True)
            yield
            c.op("act", lambda e: e.copy(out=FMC[:, :, :P], in_=b4(B[4])[:, :, :P]), reads=[B[4]], writes=[FMC])

        pending_store = []

        def flush_store():
            while pending_store:
                pending_store.pop(0)()

        def pertoken(J, gen):
            P = J["P"]
            DSK = 3

            def tok_front(t):
                sl = t % NTS
                ps = ps_s[t % 3]; eb_ = Eb[sl]; xb = Xb[sl]; lb = Lb[sl]; rb = Rb[sl]
                rp = rep8[(t // TB) % 2]
                if t % TB == 0:
                    nt = min(TB, P - t)
                    c.op("pool", lambda e: e.tensor_copy(out=rp[:, 0:nt, :].rearrange("p t (g a) -> p t g a", a=16),
                                                        in_=pq[:, :, t:t + nt].rearrange("p g t -> p t g").unsqueeze(3).to_broadcast([128, nt, 16, 16])),
                         reads=[pq], writes=[rp])
                for hf in range(2):
                    c.op("pe", lambda e: e.matmul(ps[:, hf * 128:(hf + 1) * 128], lhsT=rp[:, t % TB, hf * 128:(hf + 1) * 128], rhs=keysT[:, hf, :],
                                                  start=True, stop=True),
                         reads=[rp, keysT], writes=[ps], skip_self=True)
                c.op("act", lambda e: e.activation(out=eb_[:], in_=ps[:, 0:256], func=AF.Exp, bias=FMC[:, 3, t:t + 1]), reads=[ps, FMC], writes=[eb_])
                c.op("dve", lambda e: e.scalar_tensor_tensor(out=lb[:], in0=ps[:, 0:128], scalar=FMC[:, 0, t:t + 1], in1=eb_[:, 0:128],
                                                             op0=ALU.is_ge, op1=ALU.mult),
                     reads=[ps, FMC, eb_], writes=[lb])
                c.op("dve", lambda e: e.scalar_tensor_tensor(out=xb[:], in0=ps[:, 128:256], scalar=FMC[:, 2, t:t + 1], in1=eb_[:, 128:256],
                                                             op0=ALU.is_lt, op1=ALU.mult),
                     reads=[ps, FMC, eb_], writes=[xb])
                c.op("dve", lambda e: e.scalar_tensor_tensor(out=rb[:], in0=ps[:, 128:256], scalar=FMC[:, 1, t:t + 1], in1=xb[:],
                                                             op0=ALU.is_ge, op1=ALU.mult),
                     reads=[ps, FMC, xb], writes=[rb])

            def tok_back(t):
                sl = t % NTS
                pc = pc_s[t % 3]; lb = Lb[sl]; rb = Rb[sl]
                c.op("pe", lambda e: e.matmul(pc[:, 0:128], lhsT=rb[:], rhs=lb[:], start=True, stop=True),
                     reads=[rb, lb], writes=[pc], skip_self=True)
                c.op("act", lambda e: e.copy(out=CT[:, t * 128:(t + 1) * 128], in_=pc[:, 0:128]), reads=[pc], writes=[CT])

            for step in range(P + DSK):
                if step < P:
                    tok_front(step)
                if step - DSK >= 0:
                    tok_back(step - DSK)
                if gen is not None and step % 4 != 3:
                    next(gen, None)
                if step == 0 and gen is not None:
                    kick_w()
                if step == 6:
                    flush_store()
            if gen is not None:
                for _ in gen:
                    pass

        def sweep(J, gen):
            P = J["P"]; xs = J["xs"]; hT = J["hTp"]; pv = J["pv"]; y_dst = J["y_dst"]
            bo0 = B[2]; bo1 = B[3]
            NEBT = NEXP // 512

            def sweep_a(eb):
                wsl = nextw(euT_bf, eb)
                ba = B[eb % 2]
                mm_fm(ba, P, hT, hT, wsl)

            def sweep_b(eb):
                vsl = VS.use(pv[eb])
                ba = B[eb % 2]
                bv = b4(ba)
                ga = gA[eb % 2]; cat = CAT[eb % 2]
                c.op("act", lambda e: e.activation(out=ga[:, :, :P], in_=bv[:, :, :P], func=AF.Gelu_apprx_tanh), reads=[ba], writes=[ga])
                ctv = CT[:, :].rearrange("p (t i) -> p i t", i=128)[:, eb * 4:(eb + 1) * 4, :P]
                c.op("dve", lambda e: e.tensor_tensor(out=cat[:, :, :P], in0=ga[:, :, :P], in1=ctv, op=ALU.mult), reads=[ga, CT], writes=[cat])
                for cc in range(4):
                    for nb, bo in enumerate((bo0, bo1)):
                        first = (eb == 0 and cc == 0)
                        last = (eb == NEBT - 1 and cc == 3)
                        c.op("pe", lambda e: e.matmul(bo[:P, :], lhsT=cat[:, cc, :P], rhs=vsl[:, cc, nb * 512:(nb + 1) * 512],
                                                      start=first, stop=last),
                             reads=[cat, vsl], writes=[bo], skip_self=True)

            sweep_a(0)
            for eb in range(NEBT):
                if eb + 1 < NEBT:
                    sweep_a(eb + 1)
                sweep_b(eb)
                if gen is not None:
                    for _ in range(3 if eb % 2 == 0 else 2):
                        next(gen, None)
            if gen is not None:
                for _ in gen:
                    pass
            for nb, bo in enumerate((bo0, bo1)):
                c.op("dve", lambda e: e.tensor_tensor(out=xs[:P, nb * 512:(nb + 1) * 512], in0=bo[:P, :], in1=xs[:P, nb * 512:(nb + 1) * 512], op=ALU.add),
                     reads=[bo, xs], writes=[xs])
            c.op("act", lambda e: e.activation(out=hb[:P, :], in_=xs[:P, :], func=AF.Square, accum_out=st1[:P, 0:1]),
                 reads=[xs], writes=[hb, st1])
            rstd_chain(st1, 1, P, 1.0 / D, st3)
            c.op("dve", lambda e: e.scalar_tensor_tensor(out=yo[:P, :], in0=xs[:P, :], scalar=st3[:P, 0:1], in1=gfin_bc[:P, :],
                                                         op0=ALU.mult, op1=ALU.mult),
                 reads=[xs, st3, gfin_bc], writes=[yo])
            pending_store.append(lambda: c.dma("sp", y_dst, yo[:P, :], reads=[yo], owner=yo))

        def st_view(ap4):
            return ap4.rearrange("h (c p) v -> p h c v", p=128)

        jobs = []
        for s_ in range(NSEQ):
            for t in range(NT):
                r0 = s_ * LP + t * 128
                jobs.append(dict(P=128, x_src=xp[r0:r0 + 128, :], y_dst=y_p[r0:r0 + 128, :], cs_src=cos_p, sn_src=sin_p, col0=t * 128,
                                 sv_dst=None, s_init=("zero" if t == 0 else None),
                                 s_store=(st_view(so_p[s_]) if t == NT - 1 else None)))
        for s_ in range(NSAMP):
            r0 = s_ * DEC_SEQ
            jobs.append(dict(P=16, x_src=xsm[r0:r0 + 16, :], y_dst=y_s[r0:r0 + 16, :], cs_src=cos_s, sn_src=sin_s, col0=0,
                             sv_dst=sv_s[r0:r0 + 16, :], s_init=st_view(st_in[s_]), s_store=st_view(so_s[s_])))
        for i, J in enumerate(jobs):
            J["xs"] = xsb[i % 2]
            J["hTp"] = hTpb[i % 2]

        def plan_sweep(J):
            J["pv"] = []
            for eb in range(NEXP // 512):
                J["pv"].append(VS.add(lambda sl: sl[:].rearrange("p c d -> p (c d)"), ev_bf[eb], ev_bf))

        NJ = len(jobs)
        scr_by_name = {b_.name: b_ for b_ in (win_bf, wa_bf, wb_bf, wo_bf, wq_bf, euT_bf)}
        if wplan is not None:
            for nm, bi in wplan:
                wseq.append(plan_w(scr_by_name[nm], bi))
        for i in range(NJ):
            plan_sweep(jobs[i])
        def chain_and_routing(J):
            yield from mixer_gen(J, "chain")
            yield from routing_gen(J)

        if NJ:
            for _ in mixer_gen(jobs[0], "B"):
                pass
            for _ in chain_and_routing(jobs[0]):
                pass
        for i in range(NJ):
            J = jobs[i]
            pertoken(J, mixer_gen(jobs[i + 1], "B") if i + 1 < NJ else None)
            sweep(J, chain_and_routing(jobs[i + 1]) if i + 1 < NJ else None)
        flush_store()
        c.wait_all("sp", [vg, S])
        print("n_inst", c.n_inst, "n_wait", c.n_wait, "sems", len(c.sems))
    return nc


def _const_inputs():
    cp, sp_ = _rot_tables(np.arange(SEQ))
    cs, ss = _rot_tables(PAST + np.arange(DEC_SEQ))
    out = {"cos_p": cp, "sin_p": sp_, "cos_s": cs, "sin_s": ss}
    for P, tag in ((128, "p"), (16, "s")):
        DT, qd, kd, _, tr = _kind_consts(P)
        out["DT_" + tag] = DT; out["qd_" + tag] = qd; out["kd_" + tag] = kd; out["tr_" + tag] = tr
    return out


def _weights(inp):
    f = lambda a: np.ascontiguousarray(np.asarray(a, dtype=np.float32))
    return {
        "w_in": f(inp["w_in"][0]), "w_s": f(inp["w_s"][0]), "b_s": f(inp["b_s"][0]),
        "g_sgu": f(inp["g_sgu"][0]).reshape(1, D), "w_pa": f(inp["w_proj_a"][0]), "w_pb": f(inp["w_proj_b"][0]),
        "b_gate": f(inp["b_gate"][0]).reshape(1, 2 * D), "w_out": f(inp["w_out"][0]),
        "g_mix": f(inp["g_mix"][0]).reshape(1, D), "g_ffn": f(inp["g_ffn"][0]).reshape(1, D),
        "w_q": f(inp["w_query"][0]), "k1": f(inp["sub_keys_1"][0]), "k2": f(inp["sub_keys_2"][0]),
        "eu": f(inp["expert_u"][0]), "ev": f(inp["expert_v"][0]), "g_fin": f(inp["g_final"]).reshape(1, D),
    }


def kernel(**inp):
    xpr = np.asarray(inp["x_prompt"], dtype=np.float32)
    xsa = np.asarray(inp["x_sample"], dtype=np.float32)
    sta = np.asarray(inp["state_ret"], dtype=np.float32)[0]
    Bp = xpr.shape[0]
    per = Bp // NCORES
    nc = build_nc(NSEQ=per, NT=SEQ // 128, NSAMP=per)
    base = _weights(inp)
    base.update(_const_inputs())
    in_maps = []
    for ci in range(NCORES):
        m = dict(base)
        m["xp"] = np.ascontiguousarray(xpr[ci * per:(ci + 1) * per].reshape(per * SEQ, D))
        m["xsm"] = np.ascontiguousarray(xsa[ci * per:(ci + 1) * per].reshape(per * DEC_SEQ, D))
        m["st_in"] = np.ascontiguousarray(sta[ci * per:(ci + 1) * per])
        in_maps.append(m)
    res = run_bass_kernel_spmd(nc, in_maps, core_ids=list(range(NCORES)))
    rs = res.results
    y_prompt = np.concatenate([r["y_p"].reshape(per, SEQ, D) for r in rs], axis=0)
    y_sample = np.concatenate([r["y_s"].reshape(per, DEC_SEQ, D) for r in rs], axis=0)
    sp_o = np.concatenate([r["so_p"] for r in rs], axis=0)[None]
    ss_o = np.concatenate([r["so_s"] for r in rs], axis=0)[None]
    sv_o = np.concatenate([r["sv_s"].reshape(per, DEC_SEQ, D) for r in rs], axis=0)[None]
    return (y_prompt.astype(np.float32), y_sample.astype(np.float32), sp_o.astype(np.float32),
            ss_o.astype(np.float32), sv_o.astype(np.float32))
```

```python
from contextlib import ExitStack
import numpy as np
import concourse.bass as bass
import concourse.mybir as mybir
from concourse.bass_utils import run_bass_kernel_spmd

F32 = mybir.dt.float32
BF16 = mybir.dt.bfloat16
AF = mybir.ActivationFunctionType
ALU = mybir.AluOpType

D = 1024
SEQ = 2048
DEC_SEQ = 16
PAST = 1024
NCORES = 8
EPS = 1e-6
NEXP = 16384
IN_COLS = 10240
BORDER = [4, 5, 6, 7] + [0, 1, 2, 3] + list(range(8, 20))
NEG = -1.0e30


class Buf:
    __slots__ = ("name", "t", "lw", "rd", "dsem", "root")

    def __init__(self, name, t=None, root=None):
        self.name = name
        self.t = t
        self.lw = {}
        self.rd = {}
        self.dsem = None
        self.root = root if root is not None else self

    def __getitem__(self, k):
        return self.t[k]

    def view(self, name, ap):
        return Buf(name, ap, root=self.root)


class Ctx:
    def __init__(self, nc, stack):
        self.nc = nc
        self.stack = stack
        self.sems = {}
        self.cnt = {}
        self.engs = {"pe": nc.tensor, "act": nc.scalar, "dve": nc.vector,
                     "pool": nc.gpsimd, "sp": nc.sync}
        self.waited = {e: {} for e in self.engs}
        for e in self.engs:
            self._mksem("E_" + e)
        self.n_inst = 0
        self.n_wait = 0

    def _mksem(self, key):
        s = self.stack.enter_context(self.nc.semaphore(key))
        self.sems[key] = s
        self.cnt[key] = 0
        return s

    def sb(self, name, shape, dt=F32):
        return Buf(name, self.stack.enter_context(self.nc.sbuf_tensor(name, list(shape), dt)))

    def ps(self, name, shape, dt=F32):
        return Buf(name, self.stack.enter_context(self.nc.psum_tensor(name, list(shape), dt)))

    def dram(self, name, shape, dt, kind="Internal"):
        return Buf(name, self.nc.dram_tensor(name, list(shape), dt, kind=kind).ap())

    @staticmethod
    def _merge(dst, src):
        for k, v in src.items():
            if dst.get(k, 0) < v:
                dst[k] = v

    def _needs(self, reads, writes):
        need = {}
        for b in reads:
            self._merge(need, b.root.lw)
        for b in writes:
            self._merge(need, b.root.lw)
            self._merge(need, b.root.rd)
        return need

    def _emit_waits(self, eng, need, skip_self=False):
        e = self.engs[eng]
        w = self.waited[eng]
        for k, v in need.items():
            if skip_self and k == "E_" + eng:
                continue
            if w.get(k, 0) >= v:
                continue
            e.wait_ge(self.sems[k], v)
            w[k] = v
            self.n_wait += 1

    def _retire(self, reads, writes, tk, acc):
        k, v = tk
        for b in reads:
            b = b.root
            if b.rd.get(k, 0) < v:
                b.rd[k] = v
        for b in writes:
            b = b.root
            if acc:
                if b.lw.get(k, 0) < v:
                    b.lw[k] = v
            else:
                b.lw = {k: v}
                b.rd = {}

    def op(self, eng, fn, reads=(), writes=(), skip_self=False):
        need = self._needs(reads, writes)
        self._emit_waits(eng, need, skip_self=skip_self)
        ins = fn(self.engs[eng])
        key = "E_" + eng
        self.cnt[key] += 1
        ins.then_inc(self.sems[key], 1)
        self._retire(reads, writes, (key, self.cnt[key]), False)
        self.n_inst += 1

    def dma(self, eng, out_ap, in_ap, reads=(), writes=(), owner=None, acc=False):
        need = self._needs(reads, writes)
        self._emit_waits(eng, need)
        owner = owner.root
        if owner.dsem is None:
            owner.dsem = "D_" + owner.name
            self._mksem(owner.dsem)
        ins = self.engs[eng].dma_start(out=out_ap, in_=in_ap)
        self.cnt[owner.dsem] += 16
        ins.then_inc(self.sems[owner.dsem], 16)
        self._retire(reads, writes, (owner.dsem, self.cnt[owner.dsem]), acc)
        self.n_inst += 1

    def handoff(self, srcs, dsts):
        for d in dsts:
            d = d.root
            for s_ in srcs:
                s_ = s_.root
                self._merge(d.lw, s_.lw)
                self._merge(d.rd, s_.rd)

    def wait_all(self, eng, bufs):
        self._emit_waits(eng, self._needs(bufs, bufs))


def _gammas():
    return 1.0 - 2.0 ** (-5.0 - np.arange(4, dtype=np.float64))


def _rot_tables(pos):
    half = 128
    inv = 1.0 / (10000.0 ** np.linspace(0.0, 1.0, half, dtype=np.float32))
    ang = pos.astype(np.float32)[None, :] * inv[:, None]
    return np.cos(ang).astype(np.float32), np.sin(ang).astype(np.float32)


def _kind_consts(P):
    g = _gammas()
    idx = np.arange(P, dtype=np.float64)
    diff = idx[None, :] - idx[:, None]
    DT = np.zeros((P, 4, P), np.float32)
    for h in range(4):
        DT[:, h, :] = np.where(diff >= 0, g[h] ** np.maximum(diff, 0), 0.0) * (256 ** -0.5)
    qdec = np.zeros((128, 4, P), np.float32)
    kdec = np.zeros((P, 4), np.float32)
    for h in range(4):
        qdec[:, h, :] = (g[h] ** (idx + 1.0))[None, :]
        kdec[:, h] = g[h] ** (P - 1.0 - idx) * (256 ** -0.5)
    gL = [float(g[h] ** P) for h in range(4)]
    tril = np.tril(np.ones((P, P), np.float32))
    return DT, qdec, kdec, gL, tril


def build_nc(NSEQ=2, NT=16, NSAMP=2):
    rec = []
    _build_nc(NSEQ, NT, NSAMP, rec, None)
    return _build_nc(NSEQ, NT, NSAMP, None, rec)


def _build_nc(NSEQ, NT, NSAMP, wrec, wplan):
    nc = bass.Bass("TRN2", target_bir_lowering=False)
    LP = NT * 128

    def din(name, shape, dt=F32):
        return nc.dram_tensor(name, list(shape), dt, kind="ExternalInput").ap()

    def dout(name, shape, dt=F32):
        return nc.dram_tensor(name, list(shape), dt, kind="ExternalOutput").ap()

    xp = din("xp", [max(NSEQ, 1) * LP, D])
    xsm = din("xsm", [max(NSAMP, 1) * DEC_SEQ, D])
    st_in = din("st_in", [max(NSAMP, 1), 4, 256, 512])
    w_in = din("w_in", [D, IN_COLS])
    w_s = din("w_s", [8, 128, 128])
    b_s = din("b_s", [8, 128])
    g_sgu = din("g_sgu", [1, D])
    w_pa = din("w_pa", [D, D])
    w_pb = din("w_pb", [2 * D, D])
    b_gate = din("b_gate", [1, 2 * D])
    w_out = din("w_out", [D, D])
    g_mix = din("g_mix", [1, D])
    g_ffn = din("g_ffn", [1, D])
    w_q = din("w_q", [D, 2 * D])
    k1 = din("k1", [128, 128])
    k2 = din("k2", [128, 128])
    eu = din("eu", [NEXP, D])
    ev = din("ev", [NEXP, D])
    g_fin = din("g_fin", [1, D])
    cos_p = din("cos_p", [128, SEQ]); sin_p = din("sin_p", [128, SEQ])
    cos_s = din("cos_s", [128, DEC_SEQ]); sin_s = din("sin_s", [128, DEC_SEQ])
    cDT = {128: din("DT_p", [128, 4, 128]), 16: din("DT_s", [16, 4, 16])}
    cQD = {128: din("qd_p", [128, 4, 128]), 16: din("qd_s", [128, 4, 16])}
    cKD = {128: din("kd_p", [128, 4]), 16: din("kd_s", [16, 4])}
    cTR = {128: din("tr_p", [128, 128]), 16: din("tr_s", [16, 16])}

    y_p = dout("y_p", [max(NSEQ, 1) * LP, D])
    y_s = dout("y_s", [max(NSAMP, 1) * DEC_SEQ, D])
    so_p = dout("so_p", [max(NSEQ, 1), 4, 256, 512])
    so_s = dout("so_s", [max(NSAMP, 1), 4, 256, 512])
    sv_s = dout("sv_s", [max(NSAMP, 1) * DEC_SEQ, D])

    gLs = {P: _kind_consts(P)[3] for P in (128, 16)}

    with ExitStack() as st:
        c = Ctx(nc, st)
        win_bf = c.dram("win_bf", [20, 128, 4096], BF16)
        wa_bf = c.dram("wa_bf", [2, 128, 4096], BF16)
        wb_bf = c.dram("wb_bf", [4, 128, 4096], BF16)
        wo_bf = c.dram("wo_bf", [2, 128, 4096], BF16)
        wq_bf = c.dram("wq_bf", [4, 128, 4096], BF16)
        ev_bf = c.dram("ev_bf", [NEXP // 512, 128, 4096], BF16)
        euT_bf = c.dram("euT_bf", [NEXP // 512, 128, 4096], BF16)

        identb = c.sb("identb", [128, 128], BF16)
        identf = c.sb("identf", [128, 128], F32)
        gmix_c = c.sb("gmix_c", [128, 8]); gffn_c = c.sb("gffn_c", [128, 8])
        gfin_bc = c.sb("gfin_bc", [128, D]); gsgu_bc = c.sb("gsgu_bc", [128, D])
        keysT = c.sb("keysT", [128, 2, 128], BF16)
        eps_t = c.sb("eps_t", [128, 1])
        KC = {}
        for P in (128, 16):
            KC[P] = dict(
                DT=c.sb(f"DT{P}", [P, 4, P]), QD=c.sb(f"QD{P}", [128, 4, P]), KD=c.sb(f"KD{P}", [P, 4]),
                wsT=c.sb(f"wsT{P}", [P, 8, P], BF16), bs=c.sb(f"bs{P}", [128, 8, P]))
        cosb = c.sb("cosb", [128, 128]); sinb = c.sb("sinb", [128, 128])
        S = c.sb("S", [128, 4, 2, 512])
        W = [c.sb(f"W{i}", [128, 8, 512], BF16) for i in range(3)]
        V = [c.sb(f"V{i}", [128, 4, 1024], BF16) for i in range(2)]
        xsb = [c.sb(f"xs{i}", [128, D]) for i in range(2)]
        hb = c.sb("hb", [128, D], BF16); hTm = c.sb("hTm", [128, 8, 128], BF16)
        hTpb = [c.sb(f"hTp{i}", [128, 8, 128], BF16) for i in range(2)]
        uT = c.sb("uT", [128, 8, 128], BF16); vg = c.sb("vg", [128, D]); vn = c.sb("vn", [128, D], BF16)
        qT = c.sb("qT", [128, 8, 128], BF16); qdT = c.sb("qdT", [128, 8, 128], F32)
        kT = c.sb("kT", [128, 8, 128], BF16); kdk = c.sb("kdk", [128, 4, 256], BF16)
        rq = c.sb("rq", [128, 4, 128]); rt1 = c.sb("rt1", [128, 2, 128]); rt2 = c.sb("rt2", [128, 2, 128])
        CT = c.sb("CT", [128, 16384], BF16)
        VSG = c.sb("VSG", [128, 8192], BF16)
        vr = VSG.view("vr", VSG[:, 0:2048]); sg = VSG.view("sg", VSG[:, 2048:4096])
        gs = VSG.view("gs", VSG[:, 4096:6144]); yb = VSG.view("yb", VSG[:, 6144:8192])
        bgrow = c.sb("bgrow", [1, 2 * D], BF16); ones_b = c.sb("ones_b", [1, 128], BF16)
        yaT = c.sb("yaT", [128, 8, 128], BF16)
        ybT = c.sb("ybT", [128, 16, 128], BF16)
        scT = c.sb("scT", [128, 4, 128], BF16)
        mb = vn; mT = c.sb("mT", [128, 8, 128], BF16)
        t512 = c.sb("t512", [128, 512]); t512b = rq.view("t512b", rq[:].rearrange("p a b -> p (a b)"))
        st1 = c.sb("st1", [128, 8]); st2 = c.sb("st2", [128, 8]); st3 = c.sb("st3", [128, 8])
        pq = c.sb("pq", [128, 16, 128], BF16)
        s_all = VSG.view("s_all", VSG[:, 4096:8192].bitcast(F32).rearrange("p (c n) -> p c n", n=128))
        cand = VSG.view("cand", VSG[:, 0:4096].bitcast(F32))
        tk = c.sb("tk", [128, 16, 24]); tmpm = c.sb("tmpm", [128, 256]); tmpm2 = c.sb("tmpm2", [128, 256])
        ctop = c.sb("ctop", [128, 8, 24])
        thr = c.sb("thr", [128, 8]); negc = c.sb("negc", [128, 8]); ncm = c.sb("ncm", [128, 8])
        Zs = c.sb("Zs", [128, 8]); lnZ = c.sb("lnZ", [128, 8]); junk16 = c.sb("junk16", [128, 16])
        RC = c.sb("RC", [128, 4, 128]); FMC = c.sb("FMC", [128, 4, 128])
        NTS = 4
        TB = 4
        rep8 = [c.sb(f"rep8_{i}", [128, TB, 256], BF16) for i in range(2)]
        Eb = [c.sb(f"Eb{i}", [128, 256]) for i in range(NTS)]
        Xb = [c.sb(f"Xb{i}", [128, 128]) for i in range(NTS)]
        Lb = [c.sb(f"Lb{i}", [128, 128], BF16) for i in range(NTS)]
        Rb = [c.sb(f"Rb{i}", [128, 128], BF16) for i in range(NTS)]
        gA = [c.sb("gA0", [128, 4, 128])] * 2
        CAT = [c.sb(f"CAT{i}", [128, 4, 128], BF16) for i in range(2)]
        yo = vg
        pT = [c.ps(f"pT{i}", [128, 8, 128], BF16) for i in range(2)]
        B = [c.ps(f"B{i}", [128, 512], F32) for i in range(6)]

        def b4(bank):
            return bank[:].rearrange("p (c n) -> p c n", n=128)

        def ld(eng, dst, src):
            c.dma(eng, dst[:], src, writes=[dst], owner=dst)

        ld("sp", gfin_bc, g_fin.to_broadcast([128, D])); ld("sp", gsgu_bc, g_sgu.to_broadcast([128, D]))
        c.op("pool", lambda e: e.memset(ones_b[:], 1.0), writes=[ones_b])
        c.dma("pool", bgrow[0:1, :], b_gate[:, :], writes=[bgrow], owner=bgrow)
        c.op("pool", lambda e: e.memset(eps_t[:], EPS), writes=[eps_t])
        c.op("pool", lambda e: e.memset(identf[:], 0.0), writes=[identf])
        c.op("pool", lambda e: e.affine_select(out=identf[:], in_=identf[:], pattern=[[-1, 128]],
                                               compare_op=ALU.not_equal, fill=1.0, base=0, channel_multiplier=1),
             reads=[identf], writes=[identf])
        c.op("dve", lambda e: e.tensor_copy(out=identb[:], in_=identf[:]), reads=[identf], writes=[identb])
        for gsrc, gdst in ((g_mix, gmix_c), (g_ffn, gffn_c)):
            c.dma("sp", t512[:8, 0:128], gsrc[0, :].rearrange("(k p) -> k p", p=128), writes=[t512], owner=t512)
            c.op("pe", lambda e: e.transpose(out=B[0][:, 0:8], in_=t512[:8, 0:128], identity=identf[:8, :8]),
                 reads=[t512, identf], writes=[B[0]], skip_self=True)
            c.op("dve", lambda e: e.tensor_copy(out=gdst[:, :], in_=B[0][:, 0:8]), reads=[B[0]], writes=[gdst])
        for hi, kk in enumerate((k1, k2)):
            c.dma("sp", t512[:, 0:128], kk[:, :], writes=[t512], owner=t512)
            c.op("pe", lambda e: e.transpose(out=B[0][:, 0:128], in_=t512[:, 0:128], identity=identf[:]),
                 reads=[t512, identf], writes=[B[0]], skip_self=True)
            c.op("dve", lambda e: e.tensor_copy(out=keysT[:, hi, :], in_=B[0][:, 0:128]), reads=[B[0]], writes=[keysT])
        for P in (128, 16):
            kc = KC[P]
            ld("sp", kc["DT"], cDT[P][:, :, :]); ld("sp", kc["QD"], cQD[P][:, :, :]); ld("sp", kc["KD"], cKD[P][:, :])
            c.dma("sp", kc["bs"][:], b_s[:, 0:P].unsqueeze(0).to_broadcast([128, 8, P]), writes=[kc["bs"]], owner=kc["bs"])
            c.dma("sp", t512b[:P, 0:P], cTR[P][:, :], writes=[t512b], owner=t512b)
            for g in range(8):
                c.dma("sp", t512[:P, 0:P], w_s[g, 0:P, 0:P], writes=[t512], owner=t512)
                c.op("dve", lambda e: e.tensor_tensor(out=t512[:P, 128:128 + P], in0=t512[:P, 0:P], in1=t512b[:P, 0:P], op=ALU.mult),
                     reads=[t512, t512b], writes=[t512])
                c.op("pe", lambda e: e.transpose(out=B[0][:P, 0:P], in_=t512[:P, 128:128 + P], identity=identf[:P, :P]),
                     reads=[t512, identf], writes=[B[0]], skip_self=True)
                c.op("dve", lambda e: e.tensor_copy(out=kc["wsT"][:P, g, :], in_=B[0][:P, 0:P]), reads=[B[0]], writes=[kc["wsT"]])

        def cast_w(dst, bi, src, r0, c0):
            c.dma("pool", dst[bi].rearrange("p (k c) -> p k c", c=512),
                  src[r0:r0 + D, c0:c0 + 512].rearrange("(k p) c -> p k c", p=128), writes=[dst], owner=dst, acc=True)
        for j in range(20):
            cast_w(win_bf, j, w_in, 0, j * 512)
        for nb in range(2):
            cast_w(wa_bf, nb, w_pa, 0, nb * 512)
            cast_w(wb_bf, nb * 2, w_pb, 0, nb * 512); cast_w(wb_bf, nb * 2 + 1, w_pb, D, nb * 512)
            cast_w(wo_bf, nb, w_out, 0, nb * 512)
        for j in range(4):
            cast_w(wq_bf, j, w_q, 0, j * 512)
        for eb in range(NEXP // 512):
            c.dma("pool", ev_bf[eb].rearrange("p (c d) -> p c d", d=D),
                  ev[eb * 512:(eb + 1) * 512, :].rearrange("(c p) d -> p c d", p=128), writes=[ev_bf], owner=ev_bf, acc=True)
        stg = [Buf("stgA", CT[:, 0:4096].bitcast(F32)), Buf("stgB", CT[:, 8192:12288].bitcast(F32))]
        for eb in range(NEXP // 512):
            for hf in range(2):
                c.dma("sp", stg[hf][:, :].rearrange("p (c d) -> p c d", d=D),
                      eu[eb * 512 + hf * 256: eb * 512 + (hf + 1) * 256, :].rearrange("(c p) d -> p c d", p=128),
                      writes=[stg[hf]], owner=stg[hf])
            wsl = W[eb % 2]
            for k in range(8):
                bank = B[k % 4]
                for ec in range(4):
                    src = stg[ec // 2][:, (ec % 2) * D + k * 128:(ec % 2) * D + (k + 1) * 128]
                    c.op("pe", lambda e: e.transpose(out=bank[:, ec * 128:(ec + 1) * 128], in_=src, identity=identf[:]),
                         reads=[stg[ec // 2], identf], writes=[bank], skip_self=True)
                eng = "act" if k % 2 == 0 else "dve"
                if eng == "act":
                    c.op("act", lambda e: e.copy(out=wsl[:, k, :], in_=bank[:]), reads=[bank], writes=[wsl])
                else:
                    c.op("dve", lambda e: e.tensor_copy(out=wsl[:, k, :], in_=bank[:]), reads=[bank], writes=[wsl])
            c.dma("sp", euT_bf[eb], wsl[:].rearrange("p k e -> p (k e)"),
                  reads=[wsl], writes=[euT_bf], owner=wsl, acc=True)

        c.handoff(stg, [CT])
        ps_s = [B[0], B[1], B[2]]
        pc_s = [B[3], B[4], B[5]]
        pTf = [pT[i].view(f"pTf{i}", pT[i][:].rearrange("p a b -> p (a b)").bitcast(F32)) for i in range(2)]

        class Stream:
            def __init__(self, slots):
                self.slots = slots
                self.plan = []
                self.emitted = 0

            def add(self, out_fn, src_ap, src_buf):
                self.plan.append((out_fn, src_ap, src_buf))
                return len(self.plan) - 1

            def use(self, i):
                while self.emitted <= min(i + len(self.slots) - 1, len(self.plan) - 1):
                    j = self.emitted
                    out_fn, src_ap, src_buf = self.plan[j]
                    sl = self.slots[j % len(self.slots)]
                    c.dma("sp", out_fn(sl), src_ap, reads=[src_buf], writes=[sl], owner=sl)
                    self.emitted += 1
                return self.slots[i % len(self.slots)]

        WS = Stream(W)
        VS = Stream(V)

        def plan_w(scr, bi):
            return WS.add(lambda sl: sl[:].rearrange("p k c -> p (k c)"), scr[bi], scr)

        def transposes_to(dst, src, nchunk, P, rd):
            for c0 in range(0, nchunk, 8):
                pt = pT[(c0 // 8) % 2]
                for k in range(8):
                    c.op("pe", lambda e: e.transpose(out=pt[:, k, :P], in_=src[:P, (c0 + k) * 128:(c0 + k + 1) * 128],
                                                     identity=identb[:P, :P]),
                         reads=[rd, identb], writes=[pt], skip_self=True)
                c.op("act", lambda e: e.copy(out=dst[:, c0:c0 + 8, :P], in_=pt[:, :, :P]), reads=[pt], writes=[dst])

        def transposes_gen(dst, src, nchunk, P, rd):
            for c0 in range(0, nchunk, 8):
                pt = pT[(c0 // 8) % 2]
                for k in range(8):
                    c.op("pe", lambda e: e.transpose(out=pt[:, k, :P], in_=src[:P, (c0 + k) * 128:(c0 + k + 1) * 128],
                                                     identity=identb[:P, :P]),
                         reads=[rd, identb], writes=[pt], skip_self=True)
                yield
                c.op("act", lambda e: e.copy(out=dst[:, c0:c0 + 8, :P], in_=pt[:, :, :P]), reads=[pt], writes=[dst])

        def rstd_chain(ssq, nw, P, inv_n, out, c0=0):
            c.op("act", lambda e: e.activation(out=st2[:P, c0:c0 + nw], in_=ssq[:P, c0:c0 + nw], func=AF.Sqrt, bias=eps_t[:P, :], scale=inv_n),
                 reads=[ssq, eps_t], writes=[st2])
            c.op("dve", lambda e: e.reciprocal(out=out[:P, c0:c0 + nw], in_=st2[:P, c0:c0 + nw]), reads=[st2], writes=[out])

        def rmsnorm_to_hT(P, gbc, xs, hT):
            c.op("act", lambda e: e.activation(out=hb[:P, :], in_=xs[:P, :], func=AF.Square, accum_out=st1[:P, 0:1]),
                 reads=[xs], writes=[hb, st1])
            rstd_chain(st1, 1, P, 1.0 / D, st3)
            c.op("dve", lambda e: e.tensor_scalar(out=hb[:P, :], in0=xs[:P, :], scalar1=st3[:P, 0:1], scalar2=None, op0=ALU.mult),
                 reads=[xs, st3], writes=[hb])
            pt = pT[0]
            for k in range(8):
                c.op("pe", lambda e: e.transpose(out=pt[:, k, :P], in_=hb[:P, k * 128:(k + 1) * 128], identity=identb[:P, :P]),
                     reads=[hb, identb], writes=[pt], skip_self=True)
            c.op("dve", lambda e: e.tensor_tensor(out=hT[:, :, :P], in0=pt[:, :, :P], in1=gbc[:, :].unsqueeze(2).to_broadcast([128, 8, P]), op=ALU.mult),
                 reads=[pt, gbc], writes=[hT])

        bank_rr = [0]

        def next_bank(lo=0, hi=3):
            b = B[lo + bank_rr[0] % (hi - lo)]
            bank_rr[0] += 1
            return b

        def mm_tm(bank, P, lhs, lhs_buf, nk, wsl, first=True, last=True, k0=0):
            for k in range(nk):
                c.op("pe", lambda e: e.matmul(bank[:P, :], lhsT=lhs[:, k0 + k, :P], rhs=wsl[:, k, :],
                                              start=(first and k == 0), stop=(last and k == nk - 1)),
                     reads=[lhs_buf, wsl], writes=[bank], skip_self=True)

        def mm_fm(bank, P, rhs, rhs_buf, wsl):
            bv = b4(bank)
            for cc in range(4):
                for k in range(8):
                    c.op("pe", lambda e: e.matmul(bv[:, cc, :P], lhsT=wsl[:, k, cc * 128:(cc + 1) * 128], rhs=rhs[:, k, :P],
                                                  start=(k == 0), stop=(k == 7)),
                         reads=[rhs_buf, wsl], writes=[bank], skip_self=True)
            return bv

        def mm_tm_gen(bank, P, lhs, lhs_buf, nk, wsl, first=True, last=True, k0=0):
            for k in range(nk):
                c.op("pe", lambda e: e.matmul(bank[:P, :], lhsT=lhs[:, k0 + k, :P], rhs=wsl[:, k, :],
                                              start=(first and k == 0), stop=(last and k == nk - 1)),
                     reads=[lhs_buf, wsl], writes=[bank], skip_self=True)
                if k % 4 == 3:
                    yield

        def mm_fm_gen(bank, P, rhs, rhs_buf, wsl):
            bv = b4(bank)
            for cc in range(4):
                for k in range(8):
                    c.op("pe", lambda e: e.matmul(bv[:, cc, :P], lhsT=wsl[:, k, cc * 128:(cc + 1) * 128], rhs=rhs[:, k, :P],
                                                  start=(k == 0), stop=(k == 7)),
                         reads=[rhs_buf, wsl], writes=[bank], skip_self=True)
                yield

        wseq = []
        wpos = [0]

        def nextw(scr, bi):
            if wrec is not None:
                wrec.append((scr.name, bi))
                wpos[0] += 1
                return W[wpos[0] % 3]
            sl = WS.use(wseq[wpos[0]])
            wpos[0] += 1
            return sl

        def mixer_gen(J, part):
            P = J["P"]; xs = J["xs"]; sv_dst = J["sv_dst"]; col0 = J["col0"]; hT = hTm
            kc = KC[P]
            gL = gLs[P]
            if part == "chain":
                yield from mixer_chain(J, P, xs, kc, gL)
                return
            if J["s_init"] == "zero":
                c.op("pool", lambda e: e.memset(S[:], 0.0), writes=[S])
            elif J["s_init"] is not None:
                c.dma("sp", S[:], J["s_init"], writes=[S], owner=S)
            c.dma("sp", xs[:P, :], J["x_src"], writes=[xs], owner=xs)
            c.dma("sp", cosb[:, :P], J["cs_src"][:, col0:col0 + P], writes=[cosb], owner=cosb)
            c.dma("sp", sinb[:, :P], J["sn_src"][:, col0:col0 + P], writes=[sinb], owner=sinb)
            for _ in range(8):
                yield
            rmsnorm_to_hT(P, gmix_c, xs, hTm)
            yield
            mb_rr = [0]

            def pe_part(j, out):
                wsl = nextw(win_bf, j)
                bank = pTf[mb_rr[0] % 2]
                mb_rr[0] += 1
                out.append(bank)
                if j < 2 or 4 <= j < 8:
                    yield from mm_fm_gen(bank, P, hT, hT, wsl)
                else:
                    isgate = j >= 16
                    if isgate:
                        c.op("pe", lambda e: e.matmul(bank[:P, :], lhsT=ones_b[0:1, :P], rhs=bgrow[0:1, (j - 16) * 512:(j - 15) * 512],
                                                      start=True, stop=False),
                             reads=[ones_b, bgrow], writes=[bank], skip_self=True)
                    yield from mm_tm_gen(bank, P, hT, hT, 8, wsl, first=not isgate)

            def post_part(j, bank):
                if j < 2 or 4 <= j < 8:
                    bv = b4(bank)
                    if j < 2:
                        c.op("act", lambda e: e.activation(out=uT[:, j * 4:(j + 1) * 4, :P], in_=bv[:, :, :P], func=AF.Gelu_apprx_tanh),
                             reads=[bank], writes=[uT])
                    else:
                        isq = j < 6
                        jj = (j - 4) % 2
                        x1 = bv[:, 0:4:2, :P]; x2 = bv[:, 1:4:2, :P]
                        cb = cosb[:, :P].unsqueeze(1).to_broadcast([128, 2, P])
                        sb_ = sinb[:, :P].unsqueeze(1).to_broadcast([128, 2, P])
                        dst = qT if isq else kT
                        o1 = rq[:, 0:4:2, :P] if isq else dst[:, jj * 4:(jj + 1) * 4:2, :P]
                        o2 = rq[:, 1:4:2, :P] if isq else dst[:, jj * 4 + 1:(jj + 1) * 4:2, :P]
                        wr = [rq] if isq else [dst]
                        c.op("dve", lambda e: e.tensor_tensor(out=rt1[:, :, :P], in0=x1, in1=cb, op=ALU.mult), reads=[bank, cosb], writes=[rt1])
                        c.op("dve", lambda e: e.tensor_tensor(out=rt2[:, :, :P], in0=x2, in1=sb_, op=ALU.mult), reads=[bank, sinb], writes=[rt2])
                        c.op("dve", lambda e: e.tensor_tensor(out=o1, in0=rt1[:, :, :P], in1=rt2[:, :, :P], op=ALU.subtract),
                             reads=[rt1, rt2], writes=wr)
                        c.op("dve", lambda e: e.tensor_tensor(out=rt1[:, :, :P], in0=x1, in1=sb_, op=ALU.mult), reads=[bank, sinb], writes=[rt1])
                        c.op("dve", lambda e: e.tensor_tensor(out=rt2[:, :, :P], in0=x2, in1=cb, op=ALU.mult), reads=[bank, cosb], writes=[rt2])
                        c.op("dve", lambda e: e.tensor_tensor(out=o2, in0=rt1[:, :, :P], in1=rt2[:, :, :P], op=ALU.add),
                             reads=[rt1, rt2], writes=wr)
                        if isq:
                            c.op("act", lambda e: e.copy(out=qT[:, jj * 4:(jj + 1) * 4, :P], in_=rq[:, :, :P]), reads=[rq], writes=[qT])
                            qd = kc["QD"][:, jj * 2:(jj + 1) * 2, :P].unsqueeze(2).to_broadcast([128, 2, 2, P])
                            c.op("pool", lambda e: e.tensor_tensor(out=qdT[:, jj * 4:(jj + 1) * 4, :P].rearrange("p (h t) n -> p h t n", t=2),
                                                                  in0=rq[:, :, :P].rearrange("p (h t) n -> p h t n", t=2), in1=qd, op=ALU.mult),
                                 reads=[rq, kc["QD"]], writes=[qdT])
                else:
                    if j < 4:
                        c.op("act", lambda e: e.activation(out=vg[:P, (j - 2) * 512:(j - 1) * 512], in_=bank[:P, :], func=AF.Gelu_apprx_tanh),
                             reads=[bank], writes=[vg])
                    elif j < 12:
                        c.op("act", lambda e: e.copy(out=vr[:P, (j - 8) * 512:(j - 7) * 512], in_=bank[:P, :]), reads=[bank], writes=[vr])
                    elif j < 16:
                        c.op("act", lambda e: e.activation(out=sg[:P, (j - 12) * 512:(j - 11) * 512], in_=bank[:P, :], func=AF.Silu),
                             reads=[bank], writes=[sg])
                    else:
                        c.op("act", lambda e: e.activation(out=gs[:P, (j - 16) * 512:(j - 15) * 512], in_=bank[:P, :], func=AF.Sigmoid),
                             reads=[bank], writes=[gs])
                if j == 3:
                    c.op("act", lambda e: e.activation(out=vn[:P, :], in_=vg[:P, :], func=AF.Square, accum_out=st1[:P, 0:1]),
                         reads=[vg], writes=[vn, st1])
                    rstd_chain(st1, 1, P, 1.0 / D, st3)
                    c.op("dve", lambda e: e.scalar_tensor_tensor(out=vn[:P, :], in0=vg[:P, :], scalar=st3[:P, 0:1], in1=gsgu_bc[:P, :],
                                                                 op0=ALU.mult, op1=ALU.mult),
                         reads=[vg, st3, gsgu_bc], writes=[vn])
                    if sv_dst is not None:
                        c.op("dve", lambda e: e.scalar_tensor_tensor(out=vg[:P, :], in0=vg[:P, :], scalar=st3[:P, 0:1], in1=gsgu_bc[:P, :],
                                                                     op0=ALU.mult, op1=ALU.mult),
                             reads=[vg, st3, gsgu_bc], writes=[vg])
                        c.dma("sp", sv_dst, vg[:P, :], reads=[vg], owner=vg)

            def blocks_gen(js):
                prev = None
                for j in js:
                    ob = []
                    yield from pe_part(j, ob)
                    if prev is not None:
                        post_part(*prev)
                    yield
                    prev = (j, ob[0])
                post_part(*prev)
                yield

            for _ in blocks_gen(BORDER):
                yield
            return

        def mixer_chain(J, P, xs, kc, gL):
            if True:
                pass
            for hf in range(2):
                yield
                bank = B[4 + hf]
                bv = b4(bank)
                for g4 in range(4):
                    g = hf * 4 + g4
                    c.op("pe", lambda e: e.matmul(bv[:, g4, :P], lhsT=vn[:P, g * 128:(g + 1) * 128], rhs=kc["wsT"][:P, g, :P],
                                                  start=True, stop=True),
                         reads=[vn, kc["wsT"]], writes=[bank], skip_self=True)
                yield
                c.op("dve", lambda e: e.tensor_tensor(out=rq[:, :, :P], in0=bv[:, :, :P], in1=kc["bs"][:, hf * 4:(hf + 1) * 4, :P], op=ALU.add),
                     reads=[bank, kc["bs"]], writes=[rq])
                c.op("dve", lambda e: e.tensor_tensor(out=yaT[:, hf * 4:(hf + 1) * 4, :P], in0=rq[:, :, :P], in1=uT[:, hf * 4:(hf + 1) * 4, :P],
                                                      op=ALU.mult),
                     reads=[rq, uT], writes=[yaT])
            yield
            bsc = B[5]
            bscv = b4(bsc)
            for h in range(4):
                for cc in range(2):
                    c.op("pe", lambda e: e.matmul(bscv[:P, h, :P], lhsT=kT[:, 2 * h + cc, :P], rhs=qT[:, 2 * h + cc, :P],
                                                  start=(cc == 0), stop=(cc == 1)),
                         reads=[kT, qT], writes=[bsc], skip_self=True)
            pt = pT[1]
            for k in range(8):
                c.op("pe", lambda e: e.transpose(out=pt[:P, k, :], in_=kT[:, k, :P], identity=identb[:, :]),
                     reads=[kT, identb], writes=[pt], skip_self=True)
            yield
            c.op("dve", lambda e: e.tensor_tensor(out=scT[:P, :, :P], in0=bscv[:P, :, :P], in1=kc["DT"][:P, :, :P], op=ALU.mult),
                 reads=[bsc, kc["DT"]], writes=[scT])
            for h in range(4):
                c.op("dve", lambda e: e.tensor_scalar(out=kdk[:P, h, :].rearrange("p (t d) -> p t d", t=2), in0=pt[:P, 2 * h:2 * h + 2, :],
                                                      scalar1=kc["KD"][:P, h:h + 1], scalar2=None, op0=ALU.mult),
                     reads=[pt, kc["KD"]], writes=[kdk])
            for hp in range(2):
                yield
                for h in (2 * hp, 2 * hp + 1):
                    bo = B[4 + h % 2]
                    c.op("pe", lambda e: e.matmul(bo[:P, :], lhsT=scT[:P, h, :P], rhs=vr[:P, h * 512:(h + 1) * 512], start=True, stop=False),
                         reads=[scT, vr], writes=[bo], skip_self=True)
                    for cc in range(2):
                        c.op("pe", lambda e: e.matmul(bo[:P, :], lhsT=qdT[:, 2 * h + cc, :P], rhs=S[:, h, cc, :], start=False, stop=(cc == 1)),
                             reads=[qdT, S], writes=[bo], skip_self=True)
                yield
                for h in (2 * hp, 2 * hp + 1):
                    bo = B[4 + h % 2]
                    c.op("act", lambda e: e.activation(out=yb[:P, h * 512:(h + 1) * 512], in_=bo[:P, :], func=AF.Square, accum_out=st1[:P, h:h + 1]),
                         reads=[bo], writes=[yb, st1])
                yield
                rstd_chain(st1, 2, P, 1.0 / 512, st3, c0=2 * hp)
                yield
                for h in (2 * hp, 2 * hp + 1):
                    bo = B[4 + h % 2]
                    c.op("dve", lambda e: e.scalar_tensor_tensor(out=yb[:P, h * 512:(h + 1) * 512], in0=bo[:P, :], scalar=st3[:P, h:h + 1],
                                                                 in1=sg[:P, h * 512:(h + 1) * 512], op0=ALU.mult, op1=ALU.mult),
                         reads=[bo, st3, sg], writes=[yb])
            yield
            for _ in transposes_gen(ybT, yb, 16, P, yb):
                yield
            for h in range(4):
                yield
                for cc in range(2):
                    bu = B[4 + cc]
                    c.op("pe", lambda e: e.matmul(bu[:, :], lhsT=kdk[:P, h, cc * 128:(cc + 1) * 128], rhs=vr[:P, h * 512:(h + 1) * 512],
                                                  start=True, stop=True),
                         reads=[kdk, vr], writes=[bu], skip_self=True)
                yield
                for cc in range(2):
                    bu = B[4 + cc]
                    c.op("dve", lambda e: e.scalar_tensor_tensor(out=S[:, h, cc, :], in0=S[:, h, cc, :], scalar=gL[h], in1=bu[:, :],
                                                                 op0=ALU.mult, op1=ALU.add),
                         reads=[S, bu], writes=[S])
            if J["s_store"] is not None:
                c.dma("sp", J["s_store"], S[:], reads=[S], owner=S)
            for nb in range(2):
                yield
                ba = B[4]; bb = B[5]
                wsl = nextw(wa_bf, nb)
                mm_tm(ba, P, yaT, yaT, 8, wsl)
                for kh in range(2):
                    wsl = nextw(wb_bf, nb * 2 + kh)
                    mm_tm(bb, P, ybT, ybT, 8, wsl, first=(kh == 0), last=(kh == 1), k0=kh * 8)
                    yield
                c.op("dve", lambda e: e.tensor_tensor(out=t512[:P, :], in0=ba[:P, :], in1=gs[:P, nb * 512:(nb + 1) * 512], op=ALU.mult),
                     reads=[ba, gs], writes=[t512])
                c.op("dve", lambda e: e.tensor_tensor(out=t512b[:P, :], in0=bb[:P, :], in1=gs[:P, D + nb * 512:D + (nb + 1) * 512], op=ALU.mult),
                     reads=[bb, gs], writes=[t512b])
                c.op("pool", lambda e: e.tensor_tensor(out=mb[:P, nb * 512:(nb + 1) * 512], in0=t512[:P, :], in1=t512b[:P, :], op=ALU.add),
                     reads=[t512, t512b], writes=[mb])
            yield
            yield from transposes_gen(mT, mb, 8, P, mb)
            for nb in range(2):
                yield
                bx = B[4 + nb]
                wsl = nextw(wo_bf, nb)
                mm_tm(bx, P, mT, mT, 8, wsl)
            for nb in range(2):
                yield
                bx = B[4 + nb]
                c.op("dve", lambda e: e.tensor_tensor(out=xs[:P, nb * 512:(nb + 1) * 512], in0=bx[:P, :], in1=xs[:P, nb * 512:(nb + 1) * 512], op=ALU.add),
                     reads=[bx, xs], writes=[xs])

        def routing_gen(J):
            P = J["P"]; xs = J["xs"]; hT = J["hTp"]
            rmsnorm_to_hT(P, gffn_c, xs, hT)
            yield
            for j in range(4):
                if j:
                    yield
                wsl = nextw(wq_bf, j)
                bank = next_bank(4, 6)
                bv = mm_fm(bank, P, hT, hT, wsl)
                c.op("act", lambda e: e.copy(out=pq[:].rearrange("p (t2 h) n -> p h t2 n", t2=2)[:, 2 * j:2 * j + 2, :, :P],
                                            in_=bv[:, :, :P].rearrange("p (h t2) n -> p h t2 n", t2=2)), reads=[bank], writes=[pq])
            for q4 in range(4):
                yield
                bank = B[4 + q4 % 2]
                bv = b4(bank)
                for i4 in range(4):
                    c16 = q4 * 4 + i4
                    c.op("pe", lambda e: e.matmul(bv[:P, i4, :], lhsT=pq[:, (c16 % 2) * 8 + c16 // 2, :P], rhs=keysT[:, c16 % 2, :], start=True, stop=True),
                         reads=[pq, keysT], writes=[bank], skip_self=True)
                c.op("act", lambda e: e.copy(out=s_all[:P, q4 * 4:(q4 + 1) * 4, :], in_=bv[:P, :, :]), reads=[bank], writes=[s_all])
            for c16 in range(16):
                yield
                c.op("dve", lambda e: e.max(out=tk[:P, c16, 0:8], in_=s_all[:P, c16, :]), reads=[s_all], writes=[tk])
                c.op("dve", lambda e: e.match_replace(out=tmpm[:P, 0:128], in_to_replace=tk[:P, c16, 0:8], in_values=s_all[:P, c16, :], imm_value=NEG),
                     reads=[s_all, tk], writes=[tmpm])
                c.op("dve", lambda e: e.max(out=tk[:P, c16, 8:16], in_=tmpm[:P, 0:128]), reads=[tmpm], writes=[tk])
                if c16 % 2 == 0:
                    c.op("dve", lambda e: e.match_replace(out=tmpm2[:P, 0:128], in_to_replace=tk[:P, c16, 8:16], in_values=tmpm[:P, 0:128], imm_value=NEG),
                         reads=[tmpm, tk], writes=[tmpm2])
                    c.op("dve", lambda e: e.max(out=tk[:P, c16, 16:24], in_=tmpm2[:P, 0:128]), reads=[tmpm2], writes=[tk])
            yield
            tk4 = tk[:].rearrange("p (h t) a -> p h t a", t=2)
            c.op("pool", lambda e: e.tensor_tensor(out=cand[:P, :].rearrange("p (h a b) -> p h a b", a=16, b=16),
                                                  in0=tk4[:P, :, 0, 0:16].unsqueeze(3).to_broadcast([P, 8, 16, 16]),
                                                  in1=tk4[:P, :, 1, 0:16].unsqueeze(2).to_broadcast([P, 8, 16, 16]), op=ALU.add),
                 reads=[tk], writes=[cand])
            for h in range(8):
                yield
                c.op("dve", lambda e: e.max(out=ctop[:P, h, 0:8], in_=cand[:P, h * 256:(h + 1) * 256]), reads=[cand], writes=[ctop])
                c.op("dve", lambda e: e.match_replace(out=tmpm[:P, :], in_to_replace=ctop[:P, h, 0:8], in_values=cand[:P, h * 256:(h + 1) * 256], imm_value=NEG),
                     reads=[cand, ctop], writes=[tmpm])
                c.op("dve", lambda e: e.max(out=ctop[:P, h, 8:16], in_=tmpm[:P, :]), reads=[tmpm], writes=[ctop])
                c.op("dve", lambda e: e.match_replace(out=tmpm2[:P, :], in_to_replace=ctop[:P, h, 8:16], in_values=tmpm[:P, :], imm_value=NEG),
                     reads=[tmpm, ctop], writes=[tmpm2])
                c.op("dve", lambda e: e.max(out=ctop[:P, h, 16:24], in_=tmpm2[:P, :]), reads=[tmpm2], writes=[ctop])
            yield
            c.op("dve", lambda e: e.tensor_tensor(out=thr[:P, :], in0=ctop[:P, :, 15], in1=ctop[:P, :, 16], op=ALU.add), reads=[ctop], writes=[thr])
            c.op("dve", lambda e: e.tensor_scalar(out=thr[:P, :], in0=thr[:P, :], scalar1=0.5, scalar2=None, op0=ALU.mult), reads=[thr], writes=[thr])
            c.op("dve", lambda e: e.tensor_scalar(out=ncm[:P, :], in0=ctop[:P, :, 0], scalar1=-1.0, scalar2=None, op0=ALU.mult), reads=[ctop], writes=[ncm])
            for h in range(8):
                c.op("act", lambda e: e.activation(out=junk16[:P, :], in_=ctop[:P, h, 0:16], func=AF.Exp, bias=ncm[:P, h:h + 1], accum_out=Zs[:P, h:h + 1]),
                     reads=[ctop, ncm], writes=[junk16, Zs])
            yield
            c.op("act", lambda e: e.activation(out=lnZ[:P, :], in_=Zs[:P, :], func=AF.Ln), reads=[Zs], writes=[lnZ])
            yield
            rcv = lambda r: RC[:P, r, :].rearrange("p (h a) -> p h a", a=16)
            c.op("dve", lambda e: e.tensor_tensor(out=rcv(0), in0=tk4[:P, :, 0, 0:16], in1=tk4[:P, :, 0, 1:17], op=ALU.add), reads=[tk], writes=[RC])
            c.op("dve", lambda e: e.tensor_scalar(out=RC[:P, 0, :], in0=RC[:P, 0, :], scalar1=0.5, scalar2=None, op0=ALU.mult), reads=[RC], writes=[RC])
            c.op("dve", lambda e: e.tensor_tensor(out=rcv(1), in0=thr[:P, :].unsqueeze(2).to_broadcast([P, 8, 16]), in1=tk4[:P, :, 0, 0:16], op=ALU.subtract),
                 reads=[thr, tk], writes=[RC])
            c.op("dve", lambda e: e.tensor_copy(out=rcv(2)[:, :, 0:15], in_=rcv(1)[:, :, 1:16]), reads=[RC], writes=[RC])
            c.op("dve", lambda e: e.memset(rcv(2)[:, :, 15:16], 1.0e30), reads=[RC], writes=[RC])
            c.op("dve", lambda e: e.tensor_tensor(out=negc[:P, :], in0=ncm[:P, :], in1=lnZ[:P, :], op=ALU.subtract), reads=[ncm, lnZ], writes=[negc])
            c.op("dve", lambda e: e.tensor_scalar(out=rcv(3), in0=negc[:P, :].unsqueeze(2).to_broadcast([P, 8, 16]), scalar1=0.5, scalar2=None, op0=ALU.mult),
                 reads=[negc], writes=[RC])
            yield
            for r in range(4):
                c.op("pe", lambda e: e.transpose(out=b4(B[4])[:, r, :P], in_=RC[:P, r, :], identity=identf[:P, :P]),
                     reads=[RC, identf], writes=[B[4]], skip_self=True)
            yield
            c.op("act", lambda e: e.copy(out=FMC[:, :, :P], in_=b4(B[4])[:, :, :P]), reads=[B[4]], writes=[FMC])

        pending_store = []

        def flush_store():
            while pending_store:
                pending_store.pop(0)()

        def pertoken(J, gen):
            P = J["P"]
            DSK = 3

            def tok_front(t):
                sl = t % NTS
                ps = ps_s[t % 3]; eb_ = Eb[sl]; xb = Xb[sl]; lb = Lb[sl]; rb = Rb[sl]
                rp = rep8[(t // TB) % 2]
                if t % TB == 0:
                    nt = min(TB, P - t)
                    c.op("pool", lambda e: e.tensor_copy(out=rp[:, 0:nt, :].rearrange("p t (g a) -> p t g a", a=16),
                                                        in_=pq[:, :, t:t + nt].rearrange("p g t -> p t g").unsqueeze(3).to_broadcast([128, nt, 16, 16])),
                         reads=[pq], writes=[rp])
                for hf in range(2):
                    c.op("pe", lambda e: e.matmul(ps[:, hf * 128:(hf + 1) * 128], lhsT=rp[:, t % TB, hf * 128:(hf + 1) * 128], rhs=keysT[:, hf, :],
                                                  start=True, stop=True),
                         reads=[rp, keysT], writes=[ps], skip_self=True)
                c.op("act", lambda e: e.activation(out=eb_[:], in_=ps[:, 0:256], func=AF.Exp, bias=FMC[:, 3, t:t + 1]), reads=[ps, FMC], writes=[eb_])
                c.op("dve", lambda e: e.scalar_tensor_tensor(out=lb[:], in0=ps[:, 0:128], scalar=FMC[:, 0, t:t + 1], in1=eb_[:, 0:128],
                                                             op0=ALU.is_ge, op1=ALU.mult),
                     reads=[ps, FMC, eb_], writes=[lb])
                c.op("dve", lambda e: e.scalar_tensor_tensor(out=xb[:], in0=ps[:, 128:256], scalar=FMC[:, 2, t:t + 1], in1=eb_[:, 128:256],
                                                             op0=ALU.is_lt, op1=ALU.mult),
                     reads=[ps, FMC, eb_], writes=[xb])
                c.op("dve", lambda e: e.scalar_tensor_tensor(out=rb[:], in0=ps[:, 128:256], scalar=FMC[:, 1, t:t + 1], in1=xb[:],
                                                             op0=ALU.is_ge, op1=ALU.mult),
                     reads=[ps, FMC, xb], writes=[rb])

            def tok_back(t):
                sl = t % NTS
                pc = pc_s[t % 3]; lb = Lb[sl]; rb = Rb[sl]
                c.op("pe", lambda e: e.matmul(pc[:, 0:128], lhsT=rb[:], rhs=lb[:], start=True, stop=True),
                     reads=[rb, lb], writes=[pc], skip_self=True)
                c.op("act", lambda e: e.copy(out=CT[:, t * 128:(t + 1) * 128], in_=pc[:, 0:128]), reads=[pc], writes=[CT])

            for step in range(P + DSK):
                if step < P:
                    tok_front(step)
                if step - DSK >= 0:
                    tok_back(step - DSK)
                if gen is not None and step % 4 != 3:
                    next(gen, None)
                if step == 6:
                    flush_store()
                if step == max(7, P - 30):
                    VS.use(J["pv"][0])
            if gen is not None:
                for _ in gen:
                    pass

        def sweep(J, gen):
            P = J["P"]; xs = J["xs"]; hT = J["hTp"]; pv = J["pv"]; y_dst = J["y_dst"]
            bo0 = B[2]; bo1 = B[3]
            NEBT = NEXP // 512

            def sweep_a(eb):
                wsl = nextw(euT_bf, eb)
                ba = B[eb % 2]
                mm_fm(ba, P, hT, hT, wsl)

            def sweep_b(eb):
                vsl = VS.use(pv[eb])
                ba = B[eb % 2]
                bv = b4(ba)
                ga = gA[eb % 2]; cat = CAT[eb % 2]
                c.op("act", lambda e: e.activation(out=ga[:, :, :P], in_=bv[:, :, :P], func=AF.Gelu_apprx_tanh), reads=[ba], writes=[ga])
                ctv = CT[:, :].rearrange("p (t i) -> p i t", i=128)[:, eb * 4:(eb + 1) * 4, :P]
                c.op("dve", lambda e: e.tensor_tensor(out=cat[:, :, :P], in0=ga[:, :, :P], in1=ctv, op=ALU.mult), reads=[ga, CT], writes=[cat])
                for cc in range(4):
                    for nb, bo in enumerate((bo0, bo1)):
                        first = (eb == 0 and cc == 0)
                        last = (eb == NEBT - 1 and cc == 3)
                        c.op("pe", lambda e: e.matmul(bo[:P, :], lhsT=cat[:, cc, :P], rhs=vsl[:, cc, nb * 512:(nb + 1) * 512],
                                                      start=first, stop=last),
                             reads=[cat, vsl], writes=[bo], skip_self=True)

            sweep_a(0)
            for eb in range(NEBT):
                if eb + 1 < NEBT:
                    sweep_a(eb + 1)
                sweep_b(eb)
                if gen is not None:
                    for _ in range(3 if eb % 2 == 0 else 2):
                        next(gen, None)
            if gen is not None:
                for _ in gen:
                    pass
            for nb, bo in enumerate((bo0, bo1)):
                c.op("dve", lambda e: e.tensor_tensor(out=xs[:P, nb * 512:(nb + 1) * 512], in0=bo[:P, :], in1=xs[:P, nb * 512:(nb + 1) * 512], op=ALU.add),
                     reads=[bo, xs], writes=[xs])
            c.op("act", lambda e: e.activation(out=hb[:P, :], in_=xs[:P, :], func=AF.Square, accum_out=st1[:P, 0:1]),
                 reads=[xs], writes=[hb, st1])
            rstd_chain(st1, 1, P, 1.0 / D, st3)
            c.op("dve", lambda e: e.scalar_tensor_tensor(out=yo[:P, :], in0=xs[:P, :], scalar=st3[:P, 0:1], in1=gfin_bc[:P, :],
                                                         op0=ALU.mult, op1=ALU.mult),
                 reads=[xs, st3, gfin_bc], writes=[yo])
            pending_store.append(lambda: c.dma("sp", y_dst, yo[:P, :], reads=[yo], owner=yo))

        def st_view(ap4):
            return ap4.rearrange("h (c p) v -> p h c v", p=128)

        jobs = []
        for s_ in range(NSEQ):
            for t in range(NT):
                r0 = s_ * LP + t * 128
                jobs.append(dict(P=128, x_src=xp[r0:r0 + 128, :], y_dst=y_p[r0:r0 + 128, :], cs_src=cos_p, sn_src=sin_p, col0=t * 128,
                                 sv_dst=None, s_init=("zero" if t == 0 else None),
                                 s_store=(st_view(so_p[s_]) if t == NT - 1 else None)))
        for s_ in range(NSAMP):
            r0 = s_ * DEC_SEQ
            jobs.append(dict(P=16, x_src=xsm[r0:r0 + 16, :], y_dst=y_s[r0:r0 + 16, :], cs_src=cos_s, sn_src=sin_s, col0=0,
                             sv_dst=sv_s[r0:r0 + 16, :], s_init=st_view(st_in[s_]), s_store=st_view(so_s[s_])))
        for i, J in enumerate(jobs):
            J["xs"] = xsb[i % 2]
            J["hTp"] = hTpb[i % 2]

        def plan_sweep(J):
            J["pv"] = []
            for eb in range(NEXP // 512):
                J["pv"].append(VS.add(lambda sl: sl[:].rearrange("p c d -> p (c d)"), ev_bf[eb], ev_bf))

        NJ = len(jobs)
        scr_by_name = {b_.name: b_ for b_ in (win_bf, wa_bf, wb_bf, wo_bf, wq_bf, euT_bf)}
        if wplan is not None:
            for nm, bi in wplan:
                wseq.append(plan_w(scr_by_name[nm], bi))
        for i in range(NJ):
            plan_sweep(jobs[i])
        def chain_and_routing(J):
            yield from mixer_gen(J, "chain")
            yield from routing_gen(J)

        if NJ:
            for _ in mixer_gen(jobs[0], "B"):
                pass
            for _ in chain_and_routing(jobs[0]):
                pass
        for i in range(NJ):
            J = jobs[i]
            pertoken(J, mixer_gen(jobs[i + 1], "B") if i + 1 < NJ else None)
            sweep(J, chain_and_routing(jobs[i + 1]) if i + 1 < NJ else None)
        flush_store()
        c.wait_all("sp", [vg, S])
        print("n_inst", c.n_inst, "n_wait", c.n_wait, "sems", len(c.sems))
    return nc


def _const_inputs():
    cp, sp_ = _rot_tables(np.arange(SEQ))
    cs, ss = _rot_tables(PAST + np.arange(DEC_SEQ))
    out = {"cos_p": cp, "sin_p": sp_, "cos_s": cs, "sin_s": ss}
    for P, tag in ((128, "p"), (16, "s")):
        DT, qd, kd, _, tr = _kind_consts(P)
        out["DT_" + tag] = DT; out["qd_" + tag] = qd; out["kd_" + tag] = kd; out["tr_" + tag] = tr
    return out


def _weights(inp):
    f = lambda a: np.ascontiguousarray(np.asarray(a, dtype=np.float32))
    return {
        "w_in": f(inp["w_in"][0]), "w_s": f(inp["w_s"][0]), "b_s": f(inp["b_s"][0]),
        "g_sgu": f(inp["g_sgu"][0]).reshape(1, D), "w_pa": f(inp["w_proj_a"][0]), "w_pb": f(inp["w_proj_b"][0]),
        "b_gate": f(inp["b_gate"][0]).reshape(1, 2 * D), "w_out": f(inp["w_out"][0]),
        "g_mix": f(inp["g_mix"][0]).reshape(1, D), "g_ffn": f(inp["g_ffn"][0]).reshape(1, D),
        "w_q": f(inp["w_query"][0]), "k1": f(inp["sub_keys_1"][0]), "k2": f(inp["sub_keys_2"][0]),
        "eu": f(inp["expert_u"][0]), "ev": f(inp["expert_v"][0]), "g_fin": f(inp["g_final"]).reshape(1, D),
    }


def kernel(**inp):
    xpr = np.asarray(inp["x_prompt"], dtype=np.float32)
    xsa = np.asarray(inp["x_sample"], dtype=np.float32)
    sta = np.asarray(inp["state_ret"], dtype=np.float32)[0]
    Bp = xpr.shape[0]
    per = Bp // NCORES
    nc = build_nc(NSEQ=per, NT=SEQ // 128, NSAMP=per)
    base = _weights(inp)
    base.update(_const_inputs())
    in_maps = []
    for ci in range(NCORES):
        m = dict(base)
        m["xp"] = np.ascontiguousarray(xpr[ci * per:(ci + 1) * per].reshape(per * SEQ, D))
        m["xsm"] = np.ascontiguousarray(xsa[ci * per:(ci + 1) * per].reshape(per * DEC_SEQ, D))
        m["st_in"] = np.ascontiguousarray(sta[ci * per:(ci + 1) * per])
        in_maps.append(m)
    res = run_bass_kernel_spmd(nc, in_maps, core_ids=list(range(NCORES)))
    rs = res.results
    y_prompt = np.concatenate([r["y_p"].reshape(per, SEQ, D) for r in rs], axis=0)
    y_sample = np.concatenate([r["y_s"].reshape(per, DEC_SEQ, D) for r in rs], axis=0)
    sp_o = np.concatenate([r["so_p"] for r in rs], axis=0)[None]
    ss_o = np.concatenate([r["so_s"] for r in rs], axis=0)[None]
    sv_o = np.concatenate([r["sv_s"].reshape(per, DEC_SEQ, D) for r in rs], axis=0)[None]
    return (y_prompt.astype(np.float32), y_sample.astype(np.float32), sp_o.astype(np.float32),
            ss_o.astype(np.float32), sv_o.astype(np.float32))
```
